# Optimizing a Trainium2 kernel written in Bass

```python
import math
import jax, jax.numpy as jnp
from jax import lax
import numpy as np

D_MODEL = 1024
BATCH = 4
SEQ = 8192
DEPTH = 1

ATTN_HEADS = 8
ATTN_HEAD_DIM = 64
ATTN_QK = ATTN_HEADS * 2 * ATTN_HEAD_DIM
ATTN_V = ATTN_HEADS * 2 * ATTN_HEAD_DIM
Q_BLOCK = 128
REL_BUCKETS = 32
REL_MAX_DIST = 128
SSM_EXPAND = 2
SSM_INNER = SSM_EXPAND * D_MODEL
SSM_HEAD_DIM = 64
SSM_HEADS = SSM_INNER // SSM_HEAD_DIM
SSM_GROUPS = 8
SSM_HEADS_PER_GROUP = SSM_HEADS // SSM_GROUPS
SSM_STATE = 128
SSM_CONV = 4
SSM_CHUNK = 128
SSM_CONV_DIM = SSM_INNER + 2 * SSM_GROUPS * SSM_STATE
FFN_DIM = ((8 * D_MODEL // 3 + 127) // 128) * 128
FFN_CONV = 3
N_BRANCHES = 2
IN_PROJ_SIZES = (ATTN_QK, ATTN_QK, ATTN_V, SSM_INNER, SSM_CONV_DIM, 2 * SSM_HEADS, N_BRANCHES * D_MODEL)
IN_PROJ_DIM = 2 * ATTN_QK + ATTN_V + SSM_INNER + SSM_CONV_DIM + 2 * SSM_HEADS + N_BRANCHES * D_MODEL
RMS_EPS = 1e-6

kernel_name = "hybrid_diffattn_bissd_convglu"


def rms_norm(x, w):
    xf = x.astype(jnp.float32)
    y = xf * lax.rsqrt(jnp.mean(xf * xf, axis=-1, keepdims=True) + RMS_EPS)
    return (y * w.astype(jnp.float32)).astype(x.dtype)


def split_columns(t, sizes):
    idx = np.cumsum(np.array(sizes[:-1])).tolist()
    return jnp.split(t, idx, axis=-1)


def dwconv_centred(u, w, b):
    width = w.shape[0]
    out = lax.conv_general_dilated(
        u, w[:, None, :].astype(u.dtype), window_strides=(1,),
        padding=[(width // 2, width - 1 - width // 2)],
        dimension_numbers=('NWC', 'WIO', 'NWC'), feature_group_count=u.shape[-1])
    return out + b.astype(u.dtype)


def t5_bucket(rel):
    half = REL_BUCKETS // 2
    max_exact = half // 2
    bucket = jnp.where(rel > 0, half, 0)
    n = jnp.abs(rel)
    nf = jnp.maximum(n, 1).astype(jnp.float32)
    large = max_exact + (jnp.log(nf / max_exact) / math.log(REL_MAX_DIST / max_exact)
                         * (half - max_exact)).astype(jnp.int32)
    large = jnp.minimum(large, half - 1)
    return bucket + jnp.where(n < max_exact, n, large)


def diff_attention(q, k, v, q_norm_w, k_norm_w, rel_bias, lam, lambda_init, subln_w):
    b, s, _ = q.shape
    q = rms_norm(q.reshape(b, s, ATTN_HEADS, 2, ATTN_HEAD_DIM), q_norm_w) * (ATTN_HEAD_DIM ** -0.5)
    k = rms_norm(k.reshape(b, s, ATTN_HEADS, 2, ATTN_HEAD_DIM), k_norm_w)
    v = v.reshape(b, s, ATTN_HEADS, 2 * ATTN_HEAD_DIM)
    nblk = s // Q_BLOCK
    q_blocks = jnp.moveaxis(q.reshape(b, nblk, Q_BLOCK, ATTN_HEADS, 2, ATTN_HEAD_DIM), 1, 0)
    kpos = jnp.arange(s, dtype=jnp.int32)
    table = rel_bias.astype(jnp.float32)

    def block(args):
        qb, start = args
        logits = jnp.einsum('bqhmd,bkhmd->bmhqk', qb, k, preferred_element_type=jnp.float32)
        qpos = start + jnp.arange(Q_BLOCK, dtype=jnp.int32)
        bias = table[t5_bucket(kpos[None, :] - qpos[:, None])]
        logits = logits + jnp.transpose(bias, (2, 0, 1))[None, None]
        p = jax.nn.softmax(logits, axis=-1)
        a = p[:, 0] - lam * p[:, 1]
        return jnp.einsum('bhqk,bkhe->bqhe', a.astype(v.dtype), v)

    starts = jnp.arange(nblk, dtype=jnp.int32) * Q_BLOCK
    o = lax.map(block, (q_blocks, starts))
    o = jnp.moveaxis(o, 0, 1).reshape(b, s, ATTN_HEADS, 2 * ATTN_HEAD_DIM)
    o = rms_norm(o, subln_w) * (1.0 - lambda_init)
    return o.reshape(b, s, ATTN_V)


def ssd_scan(xs, dt, a, bm, cm):
    b, s = xs.shape[:2]
    nc = s // SSM_CHUNK
    da = dt * a
    xdt = xs * dt[..., None]

    def to_chunks(t):
        return jnp.moveaxis(t.reshape(b, nc, SSM_CHUNK, *t.shape[2:]), 1, 0)

    mask = jnp.tril(jnp.ones((SSM_CHUNK, SSM_CHUNK), dtype=bool))[None, :, :, None, None]

    def step(state, inp):
        xc, dac, bc, cc = inp
        acs = jnp.cumsum(dac, axis=1)
        seg = acs[:, :, None] - acs[:, None, :]
        decay = jnp.exp(jnp.where(mask, seg, -jnp.inf))
        cb = jnp.einsum('blgn,bsgn->bgls', cc, bc)
        y = jnp.einsum('bgls,blsgr,bsgrp->blgrp', cb, decay, xc)
        y = y + jnp.einsum('blgn,bgrpn,blgr->blgrp', cc, state, jnp.exp(acs))
        to_end = jnp.exp(acs[:, -1:] - acs)
        state = state * jnp.exp(acs[:, -1])[..., None, None] + jnp.einsum('blgn,blgr,blgrp->bgrpn', bc, to_end, xc)
        return state, y

    init = jnp.zeros((b, SSM_GROUPS, SSM_HEADS_PER_GROUP, SSM_HEAD_DIM, SSM_STATE), jnp.float32)
    _, y = lax.scan(step, init, (to_chunks(xdt), to_chunks(da), to_chunks(bm), to_chunks(cm)))
    return jnp.moveaxis(y, 0, 1).reshape(xs.shape)


def bi_ssd(z, xbc, dt_raw, conv_w, conv_b, dt_bias_f, a_log_f, dt_bias_b, a_log_b, d_skip, norm_w):
    b, s, _ = z.shape
    gr = (SSM_GROUPS, SSM_HEADS_PER_GROUP)
    xbc = jax.nn.silu(dwconv_centred(xbc, conv_w, conv_b))
    xs, bm, cm = split_columns(xbc, (SSM_INNER, SSM_GROUPS * SSM_STATE, SSM_GROUPS * SSM_STATE))
    xs = xs.reshape(b, s, *gr, SSM_HEAD_DIM).astype(jnp.float32)
    bm = bm.reshape(b, s, SSM_GROUPS, SSM_STATE).astype(jnp.float32)
    cm = cm.reshape(b, s, SSM_GROUPS, SSM_STATE).astype(jnp.float32)
    dt_raw = dt_raw.astype(jnp.float32)
    dt_f = jax.nn.softplus(dt_raw[..., :SSM_HEADS] + dt_bias_f.astype(jnp.float32)).reshape(b, s, *gr)
    dt_b = jax.nn.softplus(dt_raw[..., SSM_HEADS:] + dt_bias_b.astype(jnp.float32)).reshape(b, s, *gr)
    a_f = -jnp.exp(a_log_f.astype(jnp.float32)).reshape(gr)
    a_b = -jnp.exp(a_log_b.astype(jnp.float32)).reshape(gr)
    y_f = ssd_scan(xs, dt_f, a_f, bm, cm)
    y_b = jnp.flip(ssd_scan(jnp.flip(xs, 1), jnp.flip(dt_b, 1), a_b, jnp.flip(bm, 1), jnp.flip(cm, 1)), 1)
    y = y_f + y_b + d_skip.astype(jnp.float32).reshape(gr)[..., None] * xs
    y = y.reshape(b, s, SSM_INNER) * jax.nn.silu(z.astype(jnp.float32))
    y = rms_norm(y.reshape(b, s, SSM_GROUPS, SSM_INNER // SSM_GROUPS), norm_w.reshape(SSM_GROUPS, -1))
    return y.reshape(b, s, SSM_INNER).astype(z.dtype)


def conv_glu(h, w_up, conv_w, conv_b, w_down):
    u = dwconv_centred(h @ w_up, conv_w, conv_b)
    gate, val = jnp.split(u, 2, axis=-1)
    return (jax.nn.silu(gate) * val) @ w_down


def setup_inputs(seed: int = 0) -> dict:
    key = jax.random.key(seed)
    ks = jax.random.split(key, 32)
    f32 = jnp.float32
    L = DEPTH

    def nrm(k, shape, scale):
        return jax.random.normal(k, shape, f32) * scale

    def gain(k, shape):
        return 1.0 + 0.02 * jax.random.normal(k, shape, f32)

    def dt_bias(k):
        u = jax.random.uniform(k, (L, SSM_HEADS), f32)
        dt = jnp.exp(u * (math.log(0.1) - math.log(0.001)) + math.log(0.001))
        return dt + jnp.log(-jnp.expm1(-dt))

    def a_log(k):
        return jnp.log(jax.random.uniform(k, (L, SSM_HEADS), f32, 1.0, 16.0))

    return {
        "x": jax.random.normal(ks[0], (BATCH, SEQ, D_MODEL), f32),
        "norm1_w": gain(ks[1], (L, D_MODEL)),
        "w_in": nrm(ks[2], (L, D_MODEL, IN_PROJ_DIM), D_MODEL ** -0.5),
        "q_norm_w": gain(ks[3], (L, ATTN_HEAD_DIM)),
        "k_norm_w": gain(ks[4], (L, ATTN_HEAD_DIM)),
        "rel_bias": nrm(ks[5], (REL_BUCKETS, ATTN_HEADS), 0.5),
        "lambda_q1": nrm(ks[6], (L, ATTN_HEAD_DIM), 0.1),
        "lambda_k1": nrm(ks[7], (L, ATTN_HEAD_DIM), 0.1),
        "lambda_q2": nrm(ks[8], (L, ATTN_HEAD_DIM), 0.1),
        "lambda_k2": nrm(ks[9], (L, ATTN_HEAD_DIM), 0.1),
        "subln_w": gain(ks[10], (L, 2 * ATTN_HEAD_DIM)),
        "w_attn_out": nrm(ks[11], (L, ATTN_V, D_MODEL), ATTN_V ** -0.5),
        "ssm_conv_w": nrm(ks[12], (L, SSM_CONV, SSM_CONV_DIM), SSM_CONV ** -0.5),
        "ssm_conv_b": nrm(ks[13], (L, SSM_CONV_DIM), 0.01),
        "dt_bias_f": dt_bias(ks[14]),
        "a_log_f": a_log(ks[15]),
        "dt_bias_b": dt_bias(ks[16]),
        "a_log_b": a_log(ks[17]),
        "d_skip": gain(ks[18], (L, SSM_HEADS)),
        "ssm_norm_w": gain(ks[19], (L, SSM_INNER)),
        "w_ssm_out": nrm(ks[20], (L, SSM_INNER, D_MODEL), SSM_INNER ** -0.5),
        "w_out": nrm(ks[21], (L, D_MODEL, D_MODEL), D_MODEL ** -0.5),
        "norm2_w": gain(ks[22], (L, D_MODEL)),
        "w_ffn_up": nrm(ks[23], (L, D_MODEL, 2 * FFN_DIM), D_MODEL ** -0.5),
        "ffn_conv_w": nrm(ks[24], (L, FFN_CONV, 2 * FFN_DIM), FFN_CONV ** -0.5),
        "ffn_conv_b": nrm(ks[25], (L, 2 * FFN_DIM), 0.01),
        "w_ffn_down": nrm(ks[26], (L, FFN_DIM, D_MODEL), FFN_DIM ** -0.5),
    }


def reference(x, norm1_w, w_in, q_norm_w, k_norm_w, rel_bias, lambda_q1, lambda_k1, lambda_q2, lambda_k2,
              subln_w, w_attn_out, ssm_conv_w, ssm_conv_b, dt_bias_f, a_log_f, dt_bias_b, a_log_b, d_skip,
              ssm_norm_w, w_ssm_out, w_out, norm2_w, w_ffn_up, ffn_conv_w, ffn_conv_b, w_ffn_down):
    for layer in range(DEPTH):
        lambda_init = 0.8 - 0.6 * math.exp(-0.3 * layer)
        h = rms_norm(x, norm1_w[layer])
        proj = h @ w_in[layer]
        q, k, v, z, xbc, dt_raw, gate_logits = split_columns(proj, IN_PROJ_SIZES)
        lam = (jnp.exp(jnp.sum(lambda_q1[layer].astype(jnp.float32) * lambda_k1[layer].astype(jnp.float32)))
               - jnp.exp(jnp.sum(lambda_q2[layer].astype(jnp.float32) * lambda_k2[layer].astype(jnp.float32)))
               + lambda_init)
        attn = diff_attention(q, k, v, q_norm_w[layer], k_norm_w[layer], rel_bias, lam, lambda_init,
                              subln_w[layer]) @ w_attn_out[layer]
        ssd = bi_ssd(z, xbc, dt_raw, ssm_conv_w[layer], ssm_conv_b[layer], dt_bias_f[layer], a_log_f[layer],
                     dt_bias_b[layer], a_log_b[layer], d_skip[layer], ssm_norm_w[layer]) @ w_ssm_out[layer]
        gate_attn, gate_ssd = jnp.split(gate_logits, N_BRANCHES, axis=-1)
        mixed = jax.nn.sigmoid(gate_attn) * attn + jax.nn.sigmoid(gate_ssd) * ssd
        x = x + mixed @ w_out[layer]
        x = x + conv_glu(rms_norm(x, norm2_w[layer]), w_ffn_up[layer], ffn_conv_w[layer], ffn_conv_b[layer],
                         w_ffn_down[layer])
    return x
```

```python
import math
from contextlib import ExitStack
import numpy as np
import concourse.bass as bass
import concourse.mybir as mybir
from concourse.bass_utils import run_bass_kernel_spmd

F32 = mybir.dt.float32
BF16 = mybir.dt.bfloat16
AF = mybir.ActivationFunctionType
ALU = mybir.AluOpType

NDSEM = 8
D = 1024
KC = 8
H = 8
NH = 32
G = 8
SSMI = 2048
XBC = 4096
FFN = 2816
FC = FFN // 128
EPS = 1e-6


class Buf:
    __slots__ = ("name", "w", "r")

    def __init__(self, name=""):
        self.name = name
        self.w = None
        self.r = []


class Op:
    __slots__ = ("eng", "fn", "deps", "raw", "signal", "sem", "val", "is_dma", "prev_dma", "done")

    def __init__(self, eng, fn, is_dma=False):
        self.eng = eng
        self.fn = fn
        self.deps = set()
        self.raw = set()
        self.signal = False
        self.sem = None
        self.val = 0
        self.is_dma = is_dma
        self.prev_dma = None
        self.done = False


class Prog:
    ENGS = ("pe", "act", "dve", "pool", "sp")

    def __init__(self, nc, sems, dsems, sync_same=False):
        self.nc = nc
        self.sems = sems
        self.dsems = dsems
        self.sync_same = sync_same
        self.q = {e: [] for e in self.ENGS}
        self.dma_hist = {e: [] for e in self.ENGS}
        self.all_dma = []
        self.cnt = {e: 0 for e in self.ENGS}
        self.dcnt = {e: 0 for e in self.ENGS}
        self.last = {e: None for e in self.ENGS}
        self.waited = {e: {} for e in self.ENGS}
        self.nops = 0

    def op(self, eng, fn, reads=(), writes=(), is_dma=False):
        o = Op(eng, fn, is_dma)
        for b in reads:
            if b.w is not None:
                o.deps.add(b.w)
                o.raw.add(b.w)
            b.r.append(o)
        for b in writes:
            if b.w is not None:
                o.deps.add(b.w)
            for r in b.r:
                o.deps.add(r)
            b.w = o
            b.r = []
        o.deps.discard(o)
        if is_dma:
            h = self.dma_hist[eng]
            if len(h) >= NDSEM:
                o.prev_dma = h[-NDSEM]
            h.append(o)
            if len(h) > 4 * NDSEM:
                del h[:2 * NDSEM]
            self.all_dma.append(o)
        self.q[eng].append(o)
        self.nops += 1
        return o

    def dma(self, eng, out, in_, reads=(), writes=()):
        return self.op(eng, lambda e: e.dma_start(out=out, in_=in_), reads, writes, is_dma=True)

    def barrier(self):
        lasts = []
        for e in self.ENGS:
            if self.q[e]:
                lasts.append(self.q[e][-1])
        dmas = list(self.all_dma)
        self.all_dma = []
        for e in self.ENGS:
            o = Op(e, None)
            for l in lasts:
                if l.fn is not None:
                    o.deps.add(l)
            for d in dmas:
                o.deps.add(d)
            self.q[e].append(o)

    def _needs_sync(self, d, ename, israw=True):
        if d.done:
            return False
        if d.is_dma:
            return True
        if d.eng != ename:
            return True
        return self.sync_same and ename != "pe" and israw

    def emit(self, block):
        for e in self.ENGS:
            for o in self.q[e]:
                for d in o.deps:
                    if self._needs_sync(d, o.eng, d in o.raw):
                        d.signal = True
                if o.prev_dma is not None and not o.prev_dma.done:
                    o.prev_dma.signal = True
        for e in self.ENGS:
            for o in self.q[e]:
                if o.sem is not None:
                    continue
                if o.is_dma:
                    i = self.dcnt[e]
                    self.dcnt[e] += 1
                    o.signal = True
                    o.sem = self.dsems[e][i % NDSEM]
                    o.val = 16 * (i // NDSEM + 1)
                elif o.signal:
                    self.cnt[e] += 1
                    o.sem = self.sems[e]
                    o.val = self.cnt[e]

        def run(ename):
            ops = self.q[ename]
            waited = self.waited[ename]

            def body(eng):
                for o in ops:
                    deps = list(o.deps)
                    if o.prev_dma is not None:
                        deps.append(o.prev_dma)
                    need = {}
                    for d in deps:
                        if not self._needs_sync(d, ename, d in o.raw or d is o.prev_dma):
                            continue
                        assert d.sem is not None, "dependency on op emitted later / never signalled"
                        k = d.sem.num
                        if need.get(k, (None, 0))[1] < d.val:
                            need[k] = (d.sem, d.val)
                    for k, (sem, val) in need.items():
                        if waited.get(k, 0) < val:
                            eng.wait_ge(sem, val)
                            waited[k] = val
                    if o.fn is None:
                        continue
                    inst = o.fn(eng)
                    if o.signal:
                        inst.then_inc(o.sem, 16 if o.is_dma else 1)
            return body

        block.tensor(run("pe"))
        block.scalar(run("act"))
        block.vector(run("dve"))
        block.gpsimd(run("pool"))
        block.sync(run("sp"))
        for e in self.ENGS:
            for o in self.q[e]:
                o.done = True
                o.deps = ()
                o.raw = ()
            self.q[e] = []


class Pool:
    def __init__(self, es, nc, name, shape, dt, n, psum=False):
        self.t = []
        for i in range(n):
            if psum:
                h = es.enter_context(nc.psum_tensor(f"{name}{i}", shape, dt))
            else:
                h = es.enter_context(nc.sbuf_tensor(f"{name}{i}", shape, dt))
            self.t.append((h, Buf(f"{name}{i}")))
        self.i = 0

    def next(self):
        r = self.t[self.i % len(self.t)]
        self.i += 1
        return r


def act(P, out, in_, func, reads, writes, **kw):
    return P.op("act", lambda e: e.activation(out=out, in_=in_, func=func, **kw), reads, writes)


def tt(P, eng, out, in0, in1, op, reads, writes):
    return P.op(eng, lambda e: e.tensor_tensor(out=out, in0=in0, in1=in1, op=op), reads, writes)


def stt(P, eng, out, in0, scalar, in1, op0, op1, reads, writes):
    return P.op(eng, lambda e: e.scalar_tensor_tensor(out=out, in0=in0, scalar=scalar, in1=in1, op0=op0, op1=op1),
                reads, writes)


def ts(P, eng, out, in0, s1, s2, op0, op1, reads, writes):
    if s2 is None:
        s2, op1 = 0.0, ALU.add
    return P.op(eng, lambda e: e.tensor_scalar(out=out, in0=in0, scalar1=s1, scalar2=s2, op0=op0, op1=op1),
                reads, writes)


def cp(P, eng, out, in_, reads, writes):
    if eng == "act":
        return P.op("act", lambda e: e.copy(out=out, in_=in_), reads, writes)
    return P.op(eng, lambda e: e.tensor_copy(out=out, in_=in_), reads, writes)


def recip(P, out, in_, reads, writes):
    return P.op("dve", lambda e: e.reciprocal(out=out, in_=in_), reads, writes)


def mmg(P, out, pairs, reads, writes, start=True, stop=True):
    pairs = list(pairs)

    def fn(e):
        n = len(pairs)
        inst = None
        for i, (l, r) in enumerate(pairs):
            inst = e.matmul(out, lhsT=l, rhs=r, start=(start and i == 0), stop=(stop and i == n - 1))
        return inst
    return P.op("pe", fn, reads, writes)


def mms(P, items, reads, writes):
    items = list(items)

    def fn(e):
        inst = None
        for (o, l, r) in items:
            inst = e.matmul(o, lhsT=l, rhs=r, start=True, stop=True)
        return inst
    return P.op("pe", fn, reads, writes)


def trs(P, items, ident, reads, writes):
    items = list(items)

    def fn(e):
        inst = None
        for (o, i) in items:
            inst = e.transpose(out=o, in_=i, identity=ident)
        return inst
    return P.op("pe", fn, reads, writes)


def build_nc(S, lam_init=0.2, debug=False):
    S2 = S // 2
    TQ = S2 + 128
    NT = S // 128
    NTQ = TQ // 128
    nc = bass.Bass("TRN2", target_bir_lowering=False)

    def din(name, shape, dt=F32):
        return nc.dram_tensor(name, list(shape), dt, kind="ExternalInput").ap()

    def dscr(name, shape, dt):
        return nc.dram_tensor(name, list(shape), dt, kind="Internal").ap()

    x_d = din("x", [S, D])
    w_q = din("w_q", [D, 1024])
    w_k = din("w_k", [D, 1024])
    w_v = din("w_v", [D, 1024])
    w_z = din("w_z", [D, SSMI])
    w_xbc = din("w_xbc", [D, XBC])
    w_dt = din("w_dt", [D, 64])
    w_gate = din("w_gate", [D, 2048])
    w_ao = din("w_ao", [1024, D])
    w_so = din("w_so", [SSMI, D])
    w_o = din("w_o", [D, D])
    w_up = din("w_up", [D, 2 * FFN])
    w_dn = din("w_dn", [FFN, D])
    n1w = din("n1w", [1, D])
    n2w = din("n2w", [1, D])
    qkw = din("qkw", [128, 2])
    sublnw = din("sublnw", [128, 1])
    lamv = din("lamv", [1, 256])
    cw5 = din("cw5", [128, 32, 5])
    cb = din("cb", [128, 32])
    fcw = din("fcw", [128, 2 * FC, 3])
    fcb = din("fcb", [128, 2 * FC])
    dtb = din("dtb", [1, 64])
    alog = din("alog", [1, 64])
    dsk = din("dsk", [1, 32])
    snw = din("snw", [1, SSMI])
    toep = din("toep", [128, H, 1152])
    farb = din("farb", [128, 2 * H])
    out_d = nc.dram_tensor("out", [S2, D], F32, kind="ExternalOutput").ap()

    hT_d = dscr("hT_d", [128, KC, S], BF16)
    qT_d = dscr("qT_d", [128, H, TQ], BF16)
    kT_d = dscr("kT_d", [128, H, S], BF16)
    v_d = dscr("v_d", [128, NT, 1024], BF16)
    sz_d = dscr("sz_d", [128, NTQ, SSMI], BF16)
    gT_d = dscr("gT_d", [128, 16, TQ], BF16)
    dt_d = dscr("dt_d", [128, NT, 64], F32)
    raw_d = dscr("raw_d", [128, 32, S + 4], BF16)
    xs_d = dscr("xs_d", [128, NT, SSMI], BF16)
    bt_d = dscr("bt_d", [128, NT, 1024], BF16)
    BT_d = dscr("BT_d", [128, G, S], BF16)
    CT_d = dscr("CT_d", [128, G, S], BF16)
    yb_d = dscr("yb_d", [128, NTQ, SSMI], F32)
    ysT_d = dscr("ysT_d", [128, 16, TQ], BF16)
    oT_d = dscr("oT_d", [128, H, TQ], BF16)
    x1_d = dscr("x1_d", [128, NTQ, D], F32)
    h2T_d = dscr("h2T_d", [128, KC, TQ + 2], BF16)
    wao_b = dscr("wao_b", [128, 8, 1024], BF16)
    wso_b = dscr("wso_b", [128, 16, 1024], BF16)
    wo_b = dscr("wo_b", [128, 8, 1024], BF16)
    wup_b = dscr("wup_b", [128, 8, 2 * FFN], BF16)
    wdn_b = dscr("wdn_b", [128, FC, 1024], BF16)
    dbg = {}
    if debug:
        for nm, shp, dt in (("dbg_hT", [128, KC, S], BF16), ("dbg_qT", [128, H, TQ], BF16),
                            ("dbg_kT", [128, H, S], BF16), ("dbg_v", [128, NT, 1024], BF16),
                            ("dbg_dt", [128, NT, 64], F32), ("dbg_xs", [128, NT, SSMI], BF16),
                            ("dbg_CT", [128, G, S], BF16), ("dbg_bt", [128, NT, 1024], BF16),
                            ("dbg_oT", [128, H, TQ], BF16), ("dbg_ysT", [128, 16, TQ], BF16),
                            ("dbg_yb", [128, NTQ, SSMI], F32), ("dbg_x1", [128, NTQ, D], F32)):
            dbg[nm] = nc.dram_tensor(nm, shp, dt, kind="ExternalOutput").ap()

    with ExitStack() as top:
        sems = {e: top.enter_context(nc.semaphore("s_" + e)) for e in Prog.ENGS}
        dsems = {e: [top.enter_context(nc.semaphore(f"d_{e}{i}")) for i in range(NDSEM)] for e in Prog.ENGS}
        P = Prog(nc, sems, dsems, sync_same=True)

        def sbt(name, shape, dt=F32):
            return top.enter_context(nc.sbuf_tensor(name, shape, dt)), Buf(name)

        ident, b_ident = sbt("ident", [128, 128], BF16)
        identf, b_identf = sbt("identf", [128, 128], F32)
        ones_bf, b_ones = sbt("ones_bf", [128, 128], BF16)
        onesf, b_onesf = sbt("onesf", [128, 128], F32)
        blk64, b_blk = sbt("blk64", [128, 128], BF16)
        mLE, b_mLE = sbt("mLE", [128, 128], F32)
        mGT, b_mGT = sbt("mGT", [128, 128], F32)
        mGE, b_mGE = sbt("mGE", [128, 128], F32)
        mLT, b_mLT = sbt("mLT", [128, 128], F32)
        cst, b_cst = sbt("cst", [128, 8], F32)
        n1bc, b_n1 = sbt("n1bc", [128, D], F32)
        n2bc, b_n2 = sbt("n2bc", [128, D], F32)
        qkw_t, b_qkw = sbt("qkw_t", [128, 2], F32)
        subw_t, b_subw = sbt("subw_t", [128, 1], F32)
        lam_t, b_lam = sbt("lam_t", [128, 256], F32)
        lam_s, b_lams = sbt("lam_s", [128, 4], F32)
        cw5_t, b_cw5 = sbt("cw5_t", [128, 32, 5], F32)
        cb_t, b_cb = sbt("cb_t", [128, 32], F32)
        fcw_t, b_fcw = sbt("fcw_t", [128, 2 * FC, 3], F32)
        fcb_t, b_fcb = sbt("fcb_t", [128, 2 * FC], F32)
        dtb_t, b_dtb = sbt("dtb_t", [128, 64], F32)
        a_t, b_a = sbt("a_t", [128, 64], F32)
        dsk_t, b_dsk = sbt("dsk_t", [128, 32], F32)
        snw_t, b_snw = sbt("snw_t", [128, SSMI], F32)
        farb_t, b_farb = sbt("farb_t", [128, 2 * H], F32)
        zeros_bf, b_zeros = sbt("zeros_bf", [128, 32, 2], BF16)

        with nc.Block() as block:
            P.op("pool", lambda e: e.memset(identf[:], 1.0), writes=[b_identf])
            P.op("pool", lambda e: e.affine_select(out=identf[:], in_=identf[:], pattern=[[-1, 128]],
                                                   compare_op=ALU.is_equal, fill=0.0, base=0, channel_multiplier=1),
                 reads=[b_identf], writes=[b_identf])
            cp(P, "dve", ident[:], identf[:], [b_identf], [b_ident])
            P.op("pool", lambda e: e.memset(ones_bf[:], 1.0), writes=[b_ones])
            P.op("pool", lambda e: e.memset(onesf[:], 1.0), writes=[b_onesf])
            P.op("pool", lambda e: e.memset(zeros_bf[:], 0.0), writes=[b_zeros])
            P.op("pool", lambda e: e.memset(blk64[:], 0.0), writes=[b_blk])
            P.op("pool", lambda e: e.memset(blk64[0:64, 0:64], 1.0), writes=[b_blk])
            P.op("pool", lambda e: e.memset(blk64[64:128, 64:128], 1.0), writes=[b_blk])
            for (m, bm, cop, cm, pat) in ((mLE, b_mLE, ALU.is_ge, -1, 1),
                                          (mGT, b_mGT, ALU.is_gt, 1, -1),
                                          (mGE, b_mGE, ALU.is_ge, 1, -1),
                                          (mLT, b_mLT, ALU.is_gt, -1, 1)):
                P.op("pool", lambda e, m=m: e.memset(m[:], 1.0), writes=[bm])
                P.op("pool", lambda e, m=m, cop=cop, cm=cm, pat=pat: e.affine_select(
                    out=m[:], in_=m[:], pattern=[[pat, 128]], compare_op=cop, fill=0.0, base=0,
                    channel_multiplier=cm), reads=[bm], writes=[bm])
            P.op("pool", lambda e: e.memset(cst[:, 0:1], EPS), writes=[b_cst])
            P.op("pool", lambda e: e.memset(cst[:, 1:2], 1.0), writes=[b_cst])
            P.dma("sp", n1bc[:], n1w[0:1, :].partition_broadcast(128), writes=[b_n1])
            P.dma("sp", n2bc[:], n2w[0:1, :].partition_broadcast(128), writes=[b_n2])
            P.dma("sp", qkw_t[:], qkw[:, :], writes=[b_qkw])
            P.dma("sp", subw_t[:], sublnw[:, :], writes=[b_subw])
            P.dma("sp", lam_t[:], lamv[0:1, :].partition_broadcast(128), writes=[b_lam])
            P.dma("sp", cw5_t[:], cw5[:, :, :], writes=[b_cw5])
            P.dma("sp", cb_t[:], cb[:, :], writes=[b_cb])
            P.dma("sp", fcw_t[:], fcw[:, :, :], writes=[b_fcw])
            P.dma("sp", fcb_t[:], fcb[:, :], writes=[b_fcb])
            P.dma("sp", dtb_t[:], dtb[0:1, :].partition_broadcast(128), writes=[b_dtb])
            P.dma("sp", a_t[:], alog[0:1, :].partition_broadcast(128), writes=[b_a])
            P.dma("sp", dsk_t[:], dsk[0:1, :].partition_broadcast(128), writes=[b_dsk])
            P.dma("sp", snw_t[:], snw[0:1, :].partition_broadcast(128), writes=[b_snw])
            P.dma("sp", farb_t[:], farb[:, :], writes=[b_farb])
            act(P, a_t[:], a_t[:], AF.Exp, [b_a], [b_a])
            ts(P, "dve", a_t[:], a_t[:], -1.0, None, ALU.mult, ALU.bypass, [b_a], [b_a])
            ts(P, "dve", qkw_t[:, 0:1], qkw_t[:, 0:1], 0.125, None, ALU.mult, ALU.bypass, [b_qkw], [b_qkw])
            ts(P, "dve", subw_t[:], subw_t[:], 1.0 - lam_init, None, ALU.mult, ALU.bypass, [b_subw], [b_subw])
            tt(P, "dve", lam_t[:, 0:64], lam_t[:, 0:64], lam_t[:, 64:128], ALU.mult, [b_lam], [b_lam])
            tt(P, "dve", lam_t[:, 128:192], lam_t[:, 128:192], lam_t[:, 192:256], ALU.mult, [b_lam], [b_lam])
            P.op("dve", lambda e: e.reduce_sum(out=lam_s[:, 0:1], in_=lam_t[:, 0:64], axis=mybir.AxisListType.X),
                 [b_lam], [b_lams])
            P.op("dve", lambda e: e.reduce_sum(out=lam_s[:, 1:2], in_=lam_t[:, 128:192], axis=mybir.AxisListType.X),
                 [b_lam], [b_lams])
            act(P, lam_s[:, 2:4], lam_s[:, 0:2], AF.Exp, [b_lams], [b_lams])
            tt(P, "dve", lam_s[:, 0:1], lam_s[:, 3:4], lam_s[:, 2:3], ALU.subtract, [b_lams], [b_lams])
            ts(P, "dve", cst[:, 2:3], lam_s[:, 0:1], -lam_init, None, ALU.add, ALU.bypass, [b_lams], [b_cst])
            P.dma("sp", raw_d[:, :, 0:2], zeros_bf[:], reads=[b_zeros])
            P.dma("sp", raw_d[:, :, S + 2:S + 4], zeros_bf[:], reads=[b_zeros])
            P.dma("sp", h2T_d[:, :, 0:2], zeros_bf[:, 0:KC, 0:2], reads=[b_zeros])
            P.barrier()
            P.emit(block)

        def norm_transpose(P, pl, src_ap, b_src, wbc, b_w, dst_ap, b_dst, evac_eng):
            junk, b_junk = pl["junk"].next()
            st, b_st = pl["stat"].next()
            hb, b_hb = pl["hb"].next()
            pT, b_pT = pl["pT"].next()
            P.op("pool", lambda e: e.memset(st[:, 0:1], 0.0), writes=[b_st])
            act(P, junk[:], src_ap, AF.Square, [b_src], [b_junk, b_st], accum_out=st[:, 0:1])
            act(P, st[:, 1:2], st[:, 0:1], AF.Sqrt, [b_st, b_cst], [b_st], bias=cst[:, 0:1], scale=1.0 / D)
            recip(P, st[:, 2:3], st[:, 1:2], [b_st], [b_st])
            stt(P, "dve", hb[:], src_ap, st[:, 2:3], wbc[:], ALU.mult, ALU.mult, [b_src, b_st, b_w], [b_hb])
            trs(P, [(pT[:, k, :], hb[:, k * 128:(k + 1) * 128]) for k in range(KC)], ident[:],
                [b_hb, b_ident], [b_pT])
            cp(P, evac_eng, dst_ap, pT[:], [b_pT], [b_dst])

        with ExitStack() as es, nc.Block() as block:
            pl = {"junk": Pool(es, nc, "junk", [128, D], F32, 1),
                  "stat": Pool(es, nc, "stat", [128, 4], F32, 4),
                  "hb": Pool(es, nc, "hb", [128, D], BF16, 2),
                  "pT": Pool(es, nc, "pT", [128, KC, 128], BF16, 2, psum=True)}
            xg = Pool(es, nc, "xg", [128, 4, D], F32, 2)
            hst = Pool(es, nc, "hst", [128, KC, 512], BF16, 2)
            xv = x_d.rearrange("(t p) d -> p t d", p=128)
            for gb in range(S // 512):
                xt, b_xt = xg.next()
                P.dma("sp", xt[:], xv[:, gb * 4:(gb + 1) * 4, :], writes=[b_xt])
                hs, b_hs = hst.next()
                for j in range(4):
                    norm_transpose(P, pl, xt[:, j, :], b_xt, n1bc, b_n1, hs[:, :, j * 128:(j + 1) * 128], b_hs,
                                   "act" if j % 2 == 0 else "dve")
                P.dma("pool", hT_d[:, :, gb * 512:(gb + 1) * 512], hs[:], reads=[b_hs])
            P.barrier()
            P.emit(block)

        with ExitStack() as es, nc.Block() as block:
            wst = Pool(es, nc, "wst", [128, KC, 1024], F32, 2)
            wbf = Pool(es, nc, "wbf", [128, KC, 1024], BF16, 2)
            hblk = Pool(es, nc, "hblk", [128, KC, 512], BF16, 3)
            psA = Pool(es, nc, "psA", [128, 512], F32, 4, psum=True)
            psB = Pool(es, nc, "psB", [128, 512], F32, 2, psum=True)
            sqp = Pool(es, nc, "sqp", [128, 512], BF16, 2)
            f32p = Pool(es, nc, "f32p", [128, 512], F32, 4)
            obf = Pool(es, nc, "obf", [128, 512], BF16, 4)
            dtp = Pool(es, nc, "dtp", [128, 64], F32, 4)
            flip = [0]

            def gemm(Wd, N, mode, tok0, tok1, epi):
                Wv = Wd.rearrange("(k p) n -> p k n", p=128)
                for g0 in range(0, N, 1024):
                    gw = min(1024, N - g0)
                    ws, b_ws = wst.next()
                    P.dma("sp", ws[:, :, 0:gw], Wv[:, :, g0:g0 + gw], writes=[b_ws])
                    wb, b_wb = wbf.next()
                    cp(P, "pool", wb[:, 0:4, 0:gw], ws[:, 0:4, 0:gw], [b_ws], [b_wb])
                    cp(P, "pool", wb[:, 4:8, 0:gw], ws[:, 4:8, 0:gw], [b_ws], [b_wb])
                    for tb in range(tok0, tok1, 512):
                        tw = min(512, tok1 - tb)
                        hb, b_hb = hblk.next()
                        P.dma("sp", hb[:, :, 0:tw], hT_d[:, :, tb:tb + tw], writes=[b_hb])
                        if mode == "feat":
                            for jc in range(gw // 128):
                                ps, b_ps = psA.next()
                                mmg(P, ps[:, 0:tw], [(wb[:, k, jc * 128:(jc + 1) * 128], hb[:, k, 0:tw]) for k in range(KC)],
                                    [b_wb, b_hb], [b_ps])
                                epi(g0 // 128 + jc, tb, tw, ps, b_ps)
                        else:
                            for t4 in range(tw // 128):
                                for c0 in range(0, gw, 512):
                                    cw = min(512, gw - c0)
                                    ps, b_ps = psA.next()
                                    mmg(P, ps[:, 0:cw], [(hb[:, k, t4 * 128:(t4 + 1) * 128], wb[:, k, c0:c0 + cw]) for k in range(KC)],
                                        [b_wb, b_hb], [b_ps])
                                    epi(g0 + c0, cw, (tb + t4 * 128) // 128, ps, b_ps)

            def mk_epi_qk(dst, wcol):
                def epi(h, tb, tw, ps, b_ps):
                    sq, b_sq = sqp.next()
                    act(P, sq[:, 0:tw], ps[:, 0:tw], AF.Square, [b_ps], [b_sq])
                    p2, b_p2 = psB.next()
                    mmg(P, p2[:, 0:tw], [(blk64[:], sq[:, 0:tw])], [b_sq, b_blk], [b_p2])
                    sd, b_sd = f32p.next()
                    act(P, sd[:, 0:tw], p2[:, 0:tw], AF.Ln, [b_p2, b_cst], [b_sd], bias=cst[:, 0:1], scale=1.0 / 64)
                    act(P, sd[:, 0:tw], sd[:, 0:tw], AF.Exp, [b_sd], [b_sd], scale=-0.5)
                    ob, b_ob = obf.next()
                    stt(P, "dve", ob[:, 0:tw], ps[:, 0:tw], qkw_t[:, wcol:wcol + 1], sd[:, 0:tw], ALU.mult, ALU.mult,
                        [b_ps, b_qkw, b_sd], [b_ob])
                    P.dma("pool", dst[:, h, tb:tb + tw], ob[:, 0:tw], reads=[b_ob])
                return epi

            def epi_raw(ch, tb, tw, ps, b_ps):
                ob, b_ob = obf.next()
                flip[0] ^= 1
                cp(P, "act" if flip[0] else "dve", ob[:, 0:tw], ps[:, 0:tw], [b_ps], [b_ob])
                P.dma("pool", raw_d[:, ch, 2 + tb:2 + tb + tw], ob[:, 0:tw], reads=[b_ob])

            def epi_gate(ch, tb, tw, ps, b_ps):
                ob, b_ob = obf.next()
                act(P, ob[:, 0:tw], ps[:, 0:tw], AF.Sigmoid, [b_ps], [b_ob])
                P.dma("pool", gT_d[:, ch, tb:tb + tw], ob[:, 0:tw], reads=[b_ob])

            def epi_v(c0, cw, t, ps, b_ps):
                ob, b_ob = obf.next()
                flip[0] ^= 1
                cp(P, "act" if flip[0] else "dve", ob[:, 0:cw], ps[:, 0:cw], [b_ps], [b_ob])
                P.dma("pool", v_d[:, t, c0:c0 + cw], ob[:, 0:cw], reads=[b_ob])

            def epi_z(c0, cw, t, ps, b_ps):
                ob, b_ob = obf.next()
                act(P, ob[:, 0:cw], ps[:, 0:cw], AF.Silu, [b_ps], [b_ob])
                P.dma("pool", sz_d[:, t, c0:c0 + cw], ob[:, 0:cw], reads=[b_ob])

            def epi_dt(c0, cw, t, ps, b_ps):
                d1, b_d1 = dtp.next()
                tt(P, "dve", d1[:], ps[:, 0:64], dtb_t[:], ALU.add, [b_ps, b_dtb], [b_d1])
                act(P, d1[:], d1[:], AF.Exp, [b_d1], [b_d1])
                act(P, d1[:], d1[:], AF.Ln, [b_d1, b_cst], [b_d1], bias=cst[:, 1:2], scale=1.0)
                P.dma("pool", dt_d[:, t, :], d1[:], reads=[b_d1])

            gemm(w_q, 1024, "feat", 0, TQ, mk_epi_qk(qT_d, 0))
            gemm(w_k, 1024, "feat", 0, S, mk_epi_qk(kT_d, 1))
            gemm(w_dt, 64, "tok", 0, S, epi_dt)
            gemm(w_v, 1024, "tok", 0, S, epi_v)
            gemm(w_gate, 2048, "feat", 0, TQ, epi_gate)
            gemm(w_z, SSMI, "tok", 0, TQ, epi_z)
            gemm(w_xbc, XBC, "feat", 0, S, epi_raw)
            P.barrier()
            P.emit(block)

        with ExitStack() as es, nc.Block() as block:
            rawp = Pool(es, nc, "rawp", [128, 32, 516], BF16, 2)
            csp = Pool(es, nc, "csp", [128, 512], BF16, 3)
            stok = Pool(es, nc, "stok", [128, 4, 3072], BF16, 1)
            sfeat = Pool(es, nc, "sfeat", [128, 16, 512], BF16, 2)
            pTp = Pool(es, nc, "pTc", [128, 4, 128], BF16, 3, psum=True)
            pcv = Pool(es, nc, "pcv", [128, 512], F32, 3, psum=True)
            dg = es.enter_context(nc.sbuf_tensor("dg", [128, 32, 5, 128], BF16))
            b_dg = Buf()
            for ch in range(32):
                tt(P, "dve", dg[:, ch, :, :], ident[:].unsqueeze(1).to_broadcast([128, 5, 128]),
                   cw5_t[:, ch, :].unsqueeze(2).to_broadcast([128, 5, 128]), ALU.mult, [b_ident, b_cw5], [b_dg])
            k = 0
            for tb in range(0, S, 512):
                rw, b_rw = rawp.next()
                for c8 in range(0, 32, 8):
                    P.dma("sp", rw[:, c8:c8 + 8, :], raw_d[:, c8:c8 + 8, tb:tb + 516], writes=[b_rw])
                stk, b_stk = stok.next()
                sft, b_sft = sfeat.next()
                for ch in range(32):
                    pc, b_pc = pcv.next()
                    mmg(P, pc[:], [(dg[:, ch, o, :], rw[:, ch, o:o + 512]) for o in range(5)], [b_dg, b_rw], [b_pc])
                    if ch < 24:
                        cs, b_cs = csp.next()
                        act(P, cs[:], pc[:], AF.Silu, [b_pc, b_cb], [b_cs], bias=cb_t[:, ch:ch + 1], scale=1.0)
                        pT, b_pT = pTp.next()
                        trs(P, [(pT[:, j, :], cs[:, j * 128:(j + 1) * 128]) for j in range(4)], ident[:],
                            [b_cs, b_ident], [b_pT])
                        cp(P, "dve", stk[:, :, ch * 128:(ch + 1) * 128], pT[:], [b_pT], [b_stk])
                        if ch >= 16:
                            cp(P, "pool", sft[:, ch - 16, :], cs[:], [b_cs], [b_sft])
                    else:
                        act(P, sft[:, ch - 16, :], pc[:], AF.Silu, [b_pc, b_cb], [b_sft], bias=cb_t[:, ch:ch + 1], scale=1.0)
                t0 = tb // 128
                P.dma("pool", xs_d[:, t0:t0 + 4, :], stk[:, :, 0:2048], reads=[b_stk])
                P.dma("pool", bt_d[:, t0:t0 + 4, :], stk[:, :, 2048:3072], reads=[b_stk])
                P.dma("pool", BT_d[:, :, tb:tb + 512], sft[:, 0:8, :], reads=[b_sft])
                P.dma("pool", CT_d[:, :, tb:tb + 512], sft[:, 8:16, :], reads=[b_sft])
            P.barrier()
            P.emit(block)

        with ExitStack() as es, nc.Block() as block:
            kTp = Pool(es, nc, "kTp", [128, S], BF16, 2)
            qTp = Pool(es, nc, "qTp", [128, TQ], BF16, 2)
            vp = Pool(es, nc, "vp", [128, NT, 128], BF16, 2)
            tpp = Pool(es, nc, "tpp", [128, 1152], F32, 2)
            psS = Pool(es, nc, "psS", [128, 2, 512], F32, 2, psum=True)
            psO = Pool(es, nc, "psO", [128, 2, 512], F32, 1, psum=True)
            psZ1 = Pool(es, nc, "psZ1", [128, 512], F32, 1, psum=True)
            psZ0 = Pool(es, nc, "psZ0", [128, 512], F32, 1, psum=True)
            ptp = Pool(es, nc, "ptp", [128, 2, 512], BF16, 6)
            ndp = Pool(es, nc, "ndp", [128, 2, 512], F32, 2)
            zap = [Pool(es, nc, f"zap{par}", [128, 512], F32, 1) for par in range(2)]
            e32 = Pool(es, nc, "e32", [128, 512], F32, 8)
            ebf = Pool(es, nc, "ebf", [128, 512], BF16, 4)
            qblocks = [(q0, min(512, TQ - q0)) for q0 in range(0, TQ, 512)]
            wcs = Pool(es, nc, "wcs", [128, 8, 512], F32, 1)
            wcb = Pool(es, nc, "wcb", [128, 8, 512], BF16, 1)
            slabs = []
            for (Wd, dstd, kk_tot, ncols) in ((w_ao, wao_b, 8, 1024), (w_so, wso_b, 16, 1024), (w_o, wo_b, 8, 1024),
                                               (w_up, wup_b, 8, 2 * FFN), (w_dn, wdn_b, FC, 1024)):
                Wv_ = Wd.rearrange("(k p) n -> p k n", p=128)
                for k0 in range(0, kk_tot, 8):
                    kk = min(8, kk_tot - k0)
                    for n0 in range(0, ncols, 512):
                        slabs.append((Wv_[:, k0:k0 + kk, n0:n0 + 512], dstd[:, k0:k0 + kk, n0:n0 + 512], kk))

            def cast_slab():
                if not slabs:
                    return
                src, dst, kk = slabs.pop(0)
                ws_, b_ws_ = wcs.next()
                P.dma("sp", ws_[:, 0:kk, :], src, writes=[b_ws_])
                wb_, b_wb_ = wcb.next()
                cp(P, "pool", wb_[:, 0:kk, :], ws_[:, 0:kk, :], [b_ws_], [b_wb_])
                P.dma("pool", dst, wb_[:, 0:kk, :], reads=[b_wb_])
            osp = Pool(es, nc, "osp", [128, 2, 512], F32, 2)
            z1sp = Pool(es, nc, "z1sp", [128, 512], F32, 2)
            DEFER = 8
            pend = [None, None, None]

            def classify(kt, q0, qw):
                d = kt * 128 - q0
                if d - (qw - 1) >= 91:
                    return "hi"
                if d + 127 <= -91:
                    return "lo"
                return "near"

            for h in range(H):
                kT, b_kT = kTp.next()
                P.dma("sp", kT[:], kT_d[:, h, :], writes=[b_kT])
                qT, b_qT = qTp.next()
                P.dma("sp", qT[:], qT_d[:, h, :], writes=[b_qT])
                vt, b_vt = vp.next()
                for t16 in range(0, NT, 16):
                    t1 = min(NT, t16 + 16)
                    P.dma("sp", vt[:, t16:t1, :], v_d[:, t16:t1, h * 128:(h + 1) * 128], writes=[b_vt])
                tp, b_tp = tpp.next()
                P.dma("sp", tp[:], toep[:, h, :], writes=[b_tp])
                for (q0, qw) in qblocks:
                    pO, b_pO = psO.next()
                    za = [zap[par].next() for par in range(2)]
                    pZ1, b_Z1 = psZ1.next()
                    sq_ = {}
                    nd_ = {}

                    def qk(kt):
                        pS, b_pS = psS.next()
                        mms(P, [(pS[:, m, 0:qw], kT[m * 64:(m + 1) * 64, kt * 128:(kt + 1) * 128],
                                 qT[m * 64:(m + 1) * 64, q0:q0 + qw]) for m in range(2)], [b_kT, b_qT], [b_pS])
                        sq_[kt] = (pS, b_pS)
                        if classify(kt, q0, qw) == "near":
                            nd, b_nd = ndp.next()
                            j0 = 512 - (kt * 128 - q0)
                            tt(P, "dve", nd[:, :, 0:qw], pS[:, :, 0:qw],
                               tp[:, j0:j0 + qw].unsqueeze(1).to_broadcast([128, 2, qw]), ALU.add, [b_pS, b_tp], [b_nd])
                            nd_[kt] = (nd, b_nd)
                    qk(0)
                    for kt in range(NT):
                        if kt + 1 < NT:
                            qk(kt + 1)
                        pS, b_pS = sq_.pop(kt)
                        pt, b_pt = ptp.next()
                        cls = classify(kt, q0, qw)
                        if cls == "hi":
                            act(P, pt[:, :, 0:qw], pS[:, :, 0:qw], AF.Exp, [b_pS, b_farb], [b_pt],
                                bias=farb_t[:, 2 * h + 1:2 * h + 2], scale=1.0)
                        elif cls == "lo":
                            act(P, pt[:, :, 0:qw], pS[:, :, 0:qw], AF.Exp, [b_pS, b_farb], [b_pt],
                                bias=farb_t[:, 2 * h:2 * h + 1], scale=1.0)
                        else:
                            nd, b_nd = nd_.pop(kt)
                            act(P, pt[:, :, 0:qw], nd[:, :, 0:qw], AF.Exp, [b_nd], [b_pt])
                        for (stage, at) in ((0, 0), (1, min(DEFER, NT - 1)), (2, min(2 * DEFER, NT - 1))):
                            if kt == at and pend[stage] is not None:
                                pend[stage]()
                                pend[stage] = None
                        pv_items = [(pO[:, m, 0:qw], vt[:, kt, :], pt[:, m, 0:qw]) for m in range(2)]

                        def pvfn(e, items=pv_items, st=(kt == 0), sp_=(kt == NT - 1)):
                            inst = None
                            for (o_, l_, r_) in items:
                                inst = e.matmul(o_, lhsT=l_, rhs=r_, start=st, stop=sp_)
                            return inst
                        P.op("pe", pvfn, [b_vt, b_pt], [b_pO])
                        mmg(P, pZ1[:, 0:qw], [(ones_bf[:], pt[:, 1, 0:qw])], [b_ones, b_pt], [b_Z1],
                            start=(kt == 0), stop=(kt == NT - 1))
                        zt, b_zt = za[kt % 2]
                        if kt < 2:
                            cp(P, "dve", zt[:, 0:qw], pt[:, 0, 0:qw], [b_pt], [b_zt])
                        else:
                            tt(P, "dve", zt[:, 0:qw], zt[:, 0:qw], pt[:, 0, 0:qw], ALU.add, [b_zt, b_pt], [b_zt])
                    def fast(qw=qw, pO=pO, b_pO=b_pO, pZ1=pZ1, b_Z1=b_Z1, za=za, st_=None):
                        oS, b_oS = osp.next()
                        cp(P, "act", oS[:, :, 0:qw], pO[:, :, 0:qw], [b_pO], [b_oS])
                        z1s, b_z1s = z1sp.next()
                        cp(P, "act", z1s[:, 0:qw], pZ1[:, 0:qw], [b_Z1], [b_z1s])
                        (z0, b_z0), (z1, b_z1) = za
                        if NT > 1:
                            tt(P, "dve", z0[:, 0:qw], z0[:, 0:qw], z1[:, 0:qw], ALU.add, [b_z0, b_z1], [b_z0])
                        pZ0, b_Z0 = psZ0.next()
                        mmg(P, pZ0[:, 0:qw], [(onesf[:], z0[:, 0:qw])], [b_onesf, b_z0], [b_Z0])
                        return oS, b_oS, z1s, b_z1s, pZ0, b_Z0
                    ctx = {}

                    def stage0(ctx=ctx, fast=fast):
                        ctx["f"] = fast()

                    def stage1(ctx=ctx, qw=qw):
                        oS, b_oS, z1s, b_z1s, pZ0, b_Z0 = ctx["f"]
                        r0, b_r0 = e32.next()
                        r1, b_r1 = e32.next()
                        act(P, r0[:, 0:qw], pZ0[:, 0:qw], AF.Ln, [b_Z0], [b_r0])
                        act(P, r1[:, 0:qw], z1s[:, 0:qw], AF.Ln, [b_z1s], [b_r1])
                        act(P, r0[:, 0:qw], r0[:, 0:qw], AF.Exp, [b_r0], [b_r0], scale=-1.0)
                        act(P, r1[:, 0:qw], r1[:, 0:qw], AF.Exp, [b_r1], [b_r1], scale=-1.0)
                        tt(P, "pool", oS[:, 0, 0:qw], oS[:, 0, 0:qw], r0[:, 0:qw], ALU.mult, [b_oS, b_r0], [b_oS])
                        tt(P, "pool", oS[:, 1, 0:qw], oS[:, 1, 0:qw], r1[:, 0:qw], ALU.mult, [b_oS, b_r1], [b_oS])
                        o32, b_o32 = e32.next()
                        stt(P, "dve", o32[:, 0:qw], oS[:, 1, 0:qw], cst[:, 2:3], oS[:, 0, 0:qw], ALU.mult, ALU.add,
                            [b_oS, b_cst], [b_o32])
                        sq, b_sq = ebf.next()
                        tt(P, "pool", sq[:, 0:qw], o32[:, 0:qw], o32[:, 0:qw], ALU.mult, [b_o32], [b_sq])
                        mmg(P, pZ0[:, 0:qw], [(ones_bf[:], sq[:, 0:qw])], [b_ones, b_sq], [b_Z0])
                        ctx["o32"] = (o32, b_o32)

                    def stage2(ctx=ctx, h=h, q0=q0, qw=qw):
                        oS, b_oS, z1s, b_z1s, pZ0, b_Z0 = ctx["f"]
                        o32, b_o32 = ctx["o32"]
                        sd, b_sd = e32.next()
                        act(P, sd[:, 0:qw], pZ0[:, 0:qw], AF.Ln, [b_Z0, b_cst], [b_sd], bias=cst[:, 0:1], scale=1.0 / 128)
                        act(P, sd[:, 0:qw], sd[:, 0:qw], AF.Exp, [b_sd], [b_sd], scale=-0.5)
                        ob, b_ob = ebf.next()
                        stt(P, "dve", ob[:, 0:qw], o32[:, 0:qw], subw_t[:, 0:1], sd[:, 0:qw], ALU.mult, ALU.mult,
                            [b_o32, b_subw, b_sd], [b_ob])
                        P.dma("pool", oT_d[:, h, q0:q0 + qw], ob[:, 0:qw], reads=[b_ob])
                    for st_i in range(3):
                        if pend[st_i] is not None:
                            pend[st_i]()
                            pend[st_i] = None
                    pend[0], pend[1], pend[2] = stage0, stage1, stage2
                    cast_slab()
            for st_i in range(3):
                if pend[st_i] is not None:
                    pend[st_i]()
                    pend[st_i] = None
            while slabs:
                cast_slab()
            P.barrier()
            P.emit(block)

        with ExitStack() as es, nc.Block() as block:
            xsp = Pool(es, nc, "xsp", [128, SSMI], BF16, 2)
            btp = Pool(es, nc, "btp", [128, 1024], BF16, 2)
            BTp = Pool(es, nc, "BTp", [128, G, 128], BF16, 2)
            CTp = Pool(es, nc, "CTp", [128, G, 128], BF16, 2)
            dtl = Pool(es, nc, "dtl", [128, 64], F32, 2)
            szp = Pool(es, nc, "szp", [128, SSMI], BF16, 2)
            ybp = Pool(es, nc, "ybp", [128, SSMI], F32, 2)
            dap = Pool(es, nc, "dap", [128, 32], F32, 2)
            exp_ = Pool(es, nc, "exp_", [128, 96], F32, 2)
            w2p = Pool(es, nc, "w2p", [128, 32], F32, 2)
            xdtp = Pool(es, nc, "xdtp", [128, SSMI], BF16, 2)
            xwp = Pool(es, nc, "xwp", [128, SSMI], BF16, 2)
            ychp = Pool(es, nc, "ychp", [128, SSMI], F32, 2)
            cbmp = Pool(es, nc, "cbmp", [128, 128], F32, 9)
            sglp = Pool(es, nc, "sglp", [128, 4, 128], F32, 3)
            decp = Pool(es, nc, "decp", [128, 4, 128], F32, 9)
            MTp = Pool(es, nc, "MTp", [128, 4, 128], BF16, 9)
            tmpp = Pool(es, nc, "tmpp", [128, 256], F32, 4)
            state, b_state = [None, None], [None, None]
            sbf, b_sbf = [None, None], [None, None]
            for dd in range(2):
                state[dd] = es.enter_context(nc.sbuf_tensor(f"state{dd}", [128, G, 256], F32))
                b_state[dd] = [Buf() for _ in range(G)]
                sbf[dd] = es.enter_context(nc.sbuf_tensor(f"sbf{dd}", [128, G, 256], BF16))
                b_sbf[dd] = Buf()
            ps_sm = Pool(es, nc, "ps_sm", [128, 96], F32, 1, psum=True)
            ps_cb = Pool(es, nc, "ps_cb", [128, 128], F32, 1, psum=True)
            ps_sg = Pool(es, nc, "ps_sg", [128, 4, 128], F32, 2, psum=True)
            ps_y = Pool(es, nc, "ps_y", [128, 2, 256], F32, 2, psum=True)
            ps_ds = Pool(es, nc, "ps_ds", [128, 256], F32, 1, psum=True)
            ps_tr = Pool(es, nc, "ps_tr", [128, 4, 128], BF16, 1, psum=True)
            ysn = Pool(es, nc, "ysn", [128, SSMI], BF16, 2)
            ystg = Pool(es, nc, "ystg", [128, 16, 128], BF16, 2)
            stg = Pool(es, nc, "stg", [128, 16], F32, 2)
            junk2 = Pool(es, nc, "junk2", [128, 256], F32, 1)

            for dd in range(2):
                P.op("pool", lambda e, dd=dd: e.memset(state[dd][:], 0.0), writes=b_state[dd])
                P.op("pool", lambda e, dd=dd: e.memset(sbf[dd][:], 0.0), writes=[b_sbf[dd]])

            def ssd_chunk(c, dirn, full):
                mA, b_mA = (mLE, b_mLE) if dirn == 0 else (mGE, b_mGE)
                mB, b_mB = (mGT, b_mGT) if dirn == 0 else (mLT, b_mLT)
                st, b_st, sb_, b_sb = state[dirn], b_state[dirn], sbf[dirn], b_sbf[dirn]
                xs, b_xs = xsp.next()
                P.dma("sp", xs[:], xs_d[:, c, :], writes=[b_xs])
                bt, b_bt = btp.next()
                P.dma("sp", bt[:], bt_d[:, c, :], writes=[b_bt])
                dtt, b_dtt = dtl.next()
                P.dma("sp", dtt[:], dt_d[:, c, :], writes=[b_dtt])
                if full:
                    BT, b_BT = BTp.next()
                    P.dma("sp", BT[:], BT_d[:, :, c * 128:(c + 1) * 128], writes=[b_BT])
                    CT, b_CT = CTp.next()
                    P.dma("sp", CT[:], CT_d[:, :, c * 128:(c + 1) * 128], writes=[b_CT])
                dts = dtt[:, dirn * 32:(dirn + 1) * 32]
                da, b_da = dap.next()
                tt(P, "dve", da[:], dts, a_t[:, dirn * 32:(dirn + 1) * 32], ALU.mult, [b_dtt, b_a], [b_da])
                psm, b_psm = ps_sm.next()
                mms(P, [(psm[:, 0:32], mA[:], da[:]), (psm[:, 32:64], mB[:], da[:]), (psm[:, 64:96], onesf[:], da[:])],
                    [b_mA, b_mB, b_onesf, b_da], [b_psm])
                ex, b_ex = exp_.next()
                act(P, ex[:], psm[:], AF.Exp, [b_psm], [b_ex])
                w2, b_w2 = w2p.next()
                tt(P, "dve", w2[:], dts, ex[:, 32:64], ALU.mult, [b_dtt, b_ex], [b_w2])
                xw, b_xw = xwp.next()
                tt(P, "pool", xw[:].rearrange("p (r j) -> p r j", r=NH), xs[:].rearrange("p (r j) -> p r j", r=NH),
                   w2[:].unsqueeze(2).to_broadcast([128, NH, 64]), ALU.mult, [b_xs, b_w2], [b_xw])
                if full:
                    xdt, b_xdt = xdtp.next()
                    tt(P, "dve", xdt[:].rearrange("p (r j) -> p r j", r=NH), xs[:].rearrange("p (r j) -> p r j", r=NH),
                       dts.unsqueeze(2).to_broadcast([128, NH, 64]), ALU.mult, [b_xs, b_dtt], [b_xdt])
                    ych, b_ych = ychp.next()
                if full:
                    stA = []
                    for g in range(G):
                        pcb, b_pcb = ps_cb.next()
                        mmg(P, pcb[:], [(BT[:, g, :], CT[:, g, :])], [b_BT, b_CT], [b_pcb])
                        cbm, b_cbm = cbmp.next()
                        tt(P, "dve", cbm[:], pcb[:], mA[:], ALU.mult, [b_pcb, b_mA], [b_cbm])
                        sgl, b_sgl = sglp.next()
                        for r in range(4):
                            act(P, sgl[:, r, :], mB[:], AF.Identity, [b_mB, b_da], [b_sgl],
                                scale=da[:, g * 4 + r:g * 4 + r + 1])
                        psg, b_psg = ps_sg.next()
                        mms(P, [(psg[:, r, :], sgl[:, r, :], mA[:]) for r in range(4)], [b_sgl, b_mA], [b_psg])
                        dec, b_dec = decp.next()
                        act(P, dec[:], psg[:], AF.Exp, [b_psg], [b_dec])
                        stA.append((cbm, b_cbm, dec, b_dec))
                    stB = []
                    for g in range(G):
                        cbm, b_cbm, dec, b_dec = stA[g]
                        MT, b_MT = MTp.next()
                        tt(P, "dve", MT[:], dec[:], cbm[:].unsqueeze(1).to_broadcast([128, 4, 128]), ALU.mult,
                           [b_dec, b_cbm], [b_MT])
                        stB.append((MT, b_MT))
                    for g in range(G):
                        MT, b_MT = stB[g]
                        py, b_py = ps_y.next()
                        mms(P, [(py[:, 0, r * 64:(r + 1) * 64], MT[:, r, :], xdt[:, (g * 4 + r) * 64:(g * 4 + r + 1) * 64])
                                for r in range(4)] + [(py[:, 1, :], CT[:, g, :], sb_[:, g, :])],
                            [b_MT, b_xdt, b_CT, b_sb], [b_py])
                        tmp, b_tmp = tmpp.next()
                        tt(P, "dve", tmp[:].rearrange("p (r j) -> p r j", r=4), py[:, 1, :].rearrange("p (r j) -> p r j", r=4),
                           ex[:, g * 4:(g + 1) * 4].unsqueeze(2).to_broadcast([128, 4, 64]), ALU.mult,
                           [b_py, b_ex], [b_tmp])
                        tt(P, "dve", ych[:, g * 256:(g + 1) * 256], py[:, 0, :], tmp[:], ALU.add, [b_py, b_tmp], [b_ych])
                for g in range(G):
                    pds, b_pds = ps_ds.next()
                    mmg(P, pds[:], [(bt[:, g * 128:(g + 1) * 128], xw[:, g * 256:(g + 1) * 256])], [b_bt, b_xw], [b_pds])
                    tt(P, "pool", st[:, g, :].rearrange("p (r j) -> p r j", r=4), st[:, g, :].rearrange("p (r j) -> p r j", r=4),
                       ex[:, 64 + g * 4:64 + (g + 1) * 4].unsqueeze(2).to_broadcast([128, 4, 64]), ALU.mult,
                       [b_st[g], b_ex], [b_st[g]])
                    tt(P, "dve", st[:, g, :], st[:, g, :], pds[:], ALU.add, [b_st[g], b_pds], [b_st[g]])
                    cp(P, "act", sb_[:, g, :], st[:, g, :], [b_st[g]], [b_sb])
                if full:
                    return ych, b_ych, xs, b_xs
                return None

            for c in range(NT - 1, NTQ - 1, -1):
                ssd_chunk(c, 1, False)
            b_ybd = [Buf() for _ in range(NTQ)]
            for c in range(NTQ - 1, -1, -1):
                ych, b_ych, xs, b_xs = ssd_chunk(c, 1, True)
                P.dma("pool", yb_d[:, c, :], ych[:], reads=[b_ych], writes=[b_ybd[c]])
            for c in range(NTQ):
                ych, b_ych, xs, b_xs = ssd_chunk(c, 0, True)
                yb, b_yb = ybp.next()
                P.dma("sp", yb[:], yb_d[:, c, :], reads=[b_ybd[c]], writes=[b_yb])
                sz, b_sz = szp.next()
                P.dma("sp", sz[:], sz_d[:, c, :], writes=[b_sz])
                tt(P, "dve", ych[:], ych[:], yb[:], ALU.add, [b_ych, b_yb], [b_ych])
                tt(P, "pool", yb[:].rearrange("p (r j) -> p r j", r=NH), xs[:].rearrange("p (r j) -> p r j", r=NH),
                   dsk_t[:].unsqueeze(2).to_broadcast([128, NH, 64]), ALU.mult, [b_xs, b_dsk], [b_yb])
                tt(P, "dve", ych[:], ych[:], yb[:], ALU.add, [b_ych, b_yb], [b_ych])
                tt(P, "pool", ych[:], ych[:], sz[:], ALU.mult, [b_ych, b_sz], [b_ych])
                sg, b_sg = stg.next()
                jk, b_jk = junk2.next()
                P.op("pool", lambda e, sg=sg: e.memset(sg[:, 0:8], 0.0), writes=[b_sg])
                for g in range(G):
                    act(P, jk[:], ych[:, g * 256:(g + 1) * 256], AF.Square, [b_ych], [b_jk, b_sg], accum_out=sg[:, g:g + 1])
                act(P, sg[:, 8:16], sg[:, 0:8], AF.Sqrt, [b_sg, b_cst], [b_sg], bias=cst[:, 0:1], scale=1.0 / 256)
                recip(P, sg[:, 8:16], sg[:, 8:16], [b_sg], [b_sg])
                tt(P, "dve", ych[:].rearrange("p (g j) -> p g j", g=G), ych[:].rearrange("p (g j) -> p g j", g=G),
                   sg[:, 8:16].unsqueeze(2).to_broadcast([128, G, 256]), ALU.mult, [b_ych, b_sg], [b_ych])
                yn, b_yn = ysn.next()
                tt(P, "pool", yn[:], ych[:], snw_t[:], ALU.mult, [b_ych, b_snw], [b_yn])
                ys_, b_ys = ystg.next()
                for q4 in range(4):
                    ptr, b_ptr = ps_tr.next()
                    trs(P, [(ptr[:, j, :], yn[:, (q4 * 4 + j) * 128:(q4 * 4 + j + 1) * 128]) for j in range(4)], ident[:],
                        [b_yn, b_ident], [b_ptr])
                    cp(P, "act", ys_[:, q4 * 4:(q4 + 1) * 4, :], ptr[:], [b_ptr], [b_ys])
                P.dma("pool", ysT_d[:, :, c * 128:(c + 1) * 128], ys_[:], reads=[b_ys])
            P.barrier()
            P.emit(block)

        with ExitStack() as es:
          wao, b_wao = es.enter_context(nc.sbuf_tensor("wao", [128, 8, 1024], BF16)), Buf()
          wso, b_wso = es.enter_context(nc.sbuf_tensor("wso", [128, 16, 1024], BF16)), Buf()
          wo, b_wo = es.enter_context(nc.sbuf_tensor("wo", [128, 8, 1024], BF16)), Buf()
          with nc.Block() as block:
            P.dma("sp", wao[:], wao_b[:, :, :], writes=[b_wao])
            for k0 in range(0, 16, 8):
                P.dma("sp", wso[:, k0:k0 + 8, :], wso_b[:, k0:k0 + 8, :], writes=[b_wso])
            P.dma("sp", wo[:], wo_b[:, :, :], writes=[b_wo])
            oTp = Pool(es, nc, "oTp", [128, 8, 512], BF16, 2)
            ysTp = Pool(es, nc, "ysTp", [128, 16, 512], BF16, 1)
            gTp = Pool(es, nc, "gTp", [128, 16, 512], BF16, 1)
            xrp = Pool(es, nc, "xrp", [128, 4, D], F32, 1)
            mixp = Pool(es, nc, "mixp", [128, 8, 512], BF16, 1)
            t32 = Pool(es, nc, "t32", [128, 512], F32, 4)
            x1p = Pool(es, nc, "x1p", [128, D], F32, 3)
            h2st = Pool(es, nc, "h2st", [128, KC, 512], BF16, 2)
            psA = Pool(es, nc, "ps4A", [128, 512], F32, 2, psum=True)
            psS_ = Pool(es, nc, "ps4S", [128, 512], F32, 2, psum=True)
            psX = Pool(es, nc, "ps4X", [128, 2, 512], F32, 1, psum=True)
            pl = {"junk": Pool(es, nc, "junk4", [128, D], F32, 1),
                  "stat": Pool(es, nc, "stat4", [128, 4], F32, 4),
                  "hb": Pool(es, nc, "hb4", [128, D], BF16, 2),
                  "pT": Pool(es, nc, "pT4", [128, KC, 128], BF16, 2, psum=True)}
            xv = x_d.rearrange("(t p) d -> p t d", p=128)
            for tb in range(0, TQ, 512):
                tw = min(512, TQ - tb)
                nt4 = tw // 128
                oT, b_oT = oTp.next()
                P.dma("sp", oT[:, :, 0:tw], oT_d[:, :, tb:tb + tw], writes=[b_oT])
                ysT, b_ysT = ysTp.next()
                P.dma("sp", ysT[:, :, 0:tw], ysT_d[:, :, tb:tb + tw], writes=[b_ysT])
                gT, b_gT = gTp.next()
                P.dma("sp", gT[:, :, 0:tw], gT_d[:, :, tb:tb + tw], writes=[b_gT])
                xr, b_xr = xrp.next()
                P.dma("sp", xr[:, 0:nt4, :], xv[:, tb // 128:tb // 128 + nt4, :], writes=[b_xr])
                mix, b_mix = mixp.next()
                for n in range(8):
                    pa, b_pa = psA.next()
                    mmg(P, pa[:, 0:tw], [(wao[:, k, n * 128:(n + 1) * 128], oT[:, k, 0:tw]) for k in range(8)],
                        [b_wao, b_oT], [b_pa])
                    pS, b_pS = psS_.next()
                    mmg(P, pS[:, 0:tw], [(wso[:, k, n * 128:(n + 1) * 128], ysT[:, k, 0:tw]) for k in range(16)],
                        [b_wso, b_ysT], [b_pS])
                    ta, b_ta = t32.next()
                    tt(P, "dve", ta[:, 0:tw], pa[:, 0:tw], gT[:, n, 0:tw], ALU.mult, [b_pa, b_gT], [b_ta])
                    tb_, b_tb = t32.next()
                    tt(P, "dve", tb_[:, 0:tw], pS[:, 0:tw], gT[:, 8 + n, 0:tw], ALU.mult, [b_pS, b_gT], [b_tb])
                    tt(P, "pool", mix[:, n, 0:tw], ta[:, 0:tw], tb_[:, 0:tw], ALU.add, [b_ta, b_tb], [b_mix])
                h2s, b_h2s = h2st.next()
                for t4 in range(nt4):
                    px, b_px = psX.next()
                    for hf in range(2):
                        mmg(P, px[:, hf, :], [(mix[:, k, t4 * 128:(t4 + 1) * 128], wo[:, k, hf * 512:(hf + 1) * 512])
                                              for k in range(8)], [b_mix, b_wo], [b_px])
                    x1, b_x1 = x1p.next()
                    tt(P, "dve", x1[:], px[:].rearrange("p a b -> p (a b)"), xr[:, t4, :], ALU.add, [b_px, b_xr], [b_x1])
                    P.dma("pool", x1_d[:, tb // 128 + t4, :], x1[:], reads=[b_x1])
                    norm_transpose(P, pl, x1[:], b_x1, n2bc, b_n2, h2s[:, :, t4 * 128:(t4 + 1) * 128], b_h2s,
                                   "act" if t4 % 2 == 0 else "dve")
                P.dma("pool", h2T_d[:, :, 1 + tb:1 + tb + tw], h2s[:, :, 0:tw], reads=[b_h2s])
            P.barrier()
            P.emit(block)

        with ExitStack() as es:
          wup, b_wup = es.enter_context(nc.sbuf_tensor("wup", [128, 8, 2 * FFN], BF16)), Buf()
          wdn, b_wdn = es.enter_context(nc.sbuf_tensor("wdn", [128, FC, 1024], BF16)), Buf()
          with nc.Block() as block:
            for n0 in range(0, 2 * FFN, 1024):
                n1 = min(2 * FFN, n0 + 1024)
                P.dma("sp", wup[:, :, n0:n1], wup_b[:, :, n0:n1], writes=[b_wup])
            for k0 in range(0, FC, 8):
                k1 = min(FC, k0 + 8)
                P.dma("sp", wdn[:, k0:k1, :], wdn_b[:, k0:k1, :], writes=[b_wdn])
            FW = 256
            h2p = Pool(es, nc, "h2p", [128, KC, FW + 2], BF16, 2)
            x1p = Pool(es, nc, "x1f", [128, 2, D], F32, 1)
            gvp = Pool(es, nc, "gvp", [128, FW], F32, 8)
            ggp = Pool(es, nc, "ggp", [128, FC, FW], BF16, 1)
            outp = Pool(es, nc, "outp", [128, D], F32, 2)
            psU = Pool(es, nc, "psU", [128, 2, 512], F32, 3, psum=True)
            psD = Pool(es, nc, "psD", [128, 2, 512], F32, 1, psum=True)
            for tb in range(0, S2, FW):
                tw = min(FW, S2 - tb)
                nt3 = tw // 128
                h2, b_h2 = h2p.next()
                P.dma("sp", h2[:, :, 0:tw + 2], h2T_d[:, :, tb:tb + tw + 2], writes=[b_h2])
                x1, b_x1 = x1p.next()
                P.dma("sp", x1[:, 0:nt3, :], x1_d[:, tb // 128:tb // 128 + nt3, :], writes=[b_x1])
                gg, b_gg = ggp.next()
                prev_f = None

                def finish_f(f, res, gg, b_gg, tw):
                    (ga, b_ga), (va, b_va) = res
                    act(P, ga[:, 0:tw], ga[:, 0:tw], AF.Silu, [b_ga], [b_ga])
                    tt(P, "pool", gg[:, f, 0:tw], ga[:, 0:tw], va[:, 0:tw], ALU.mult, [b_ga, b_va], [b_gg])
                for f in range(FC):
                    pu, b_pu = psU.next()
                    for half, ch in ((0, f), (1, FC + f)):
                        mmg(P, pu[:, half, 0:tw + 2], [(wup[:, k, ch * 128:(ch + 1) * 128], h2[:, k, 0:tw + 2]) for k in range(KC)],
                            [b_wup, b_h2], [b_pu])
                    res = []
                    for half, ch in ((0, f), (1, FC + f)):
                        a, b_a_ = gvp.next()
                        act(P, a[:, 0:tw], pu[:, half, 0:tw], AF.Identity, [b_pu, b_fcw, b_fcb], [b_a_],
                            bias=fcb_t[:, ch:ch + 1], scale=fcw_t[:, ch, 0:1])
                        res.append((a, b_a_))
                    for tap in (1, 2):
                        for half, ch in ((0, f), (1, FC + f)):
                            a, b_a_ = res[half]
                            stt(P, "dve", a[:, 0:tw], pu[:, half, tap:tw + tap], fcw_t[:, ch, tap:tap + 1], a[:, 0:tw],
                                ALU.mult, ALU.add, [b_pu, b_fcw, b_a_], [b_a_])
                    if prev_f is not None:
                        finish_f(*prev_f)
                    prev_f = (f, res, gg, b_gg, tw)
                finish_f(*prev_f)
                prev_f = None
                for t3 in range(nt3):
                    pd, b_pd = psD.next()
                    for hf in range(2):
                        mmg(P, pd[:, hf, :], [(gg[:, f, t3 * 128:(t3 + 1) * 128], wdn[:, f, hf * 512:(hf + 1) * 512])
                                              for f in range(FC)], [b_gg, b_wdn], [b_pd])
                    ot, b_ot = outp.next()
                    tt(P, "dve", ot[:], pd[:].rearrange("p a b -> p (a b)"), x1[:, t3, :], ALU.add, [b_pd, b_x1], [b_ot])
                    r0 = tb + t3 * 128
                    P.dma("pool", out_d[r0:r0 + 128, :], ot[:], reads=[b_ot])
            if debug:
                P.barrier()
                for nm, src in (("dbg_hT", hT_d), ("dbg_qT", qT_d), ("dbg_kT", kT_d), ("dbg_v", v_d), ("dbg_dt", dt_d),
                                ("dbg_xs", xs_d), ("dbg_CT", CT_d), ("dbg_bt", bt_d), ("dbg_oT", oT_d),
                                ("dbg_ysT", ysT_d), ("dbg_yb", yb_d), ("dbg_x1", x1_d)):
                    P.dma("sp", dbg[nm], src)
            P.barrier()
            P.emit(block)
    return nc


def _t5_bucket_np(rel):
    half, max_exact = 16, 8
    bucket = np.where(rel > 0, half, 0)
    n = np.abs(rel)
    nf = np.maximum(n, 1).astype(np.float32)
    large = max_exact + (np.log(nf / np.float32(max_exact)) / np.float32(math.log(128 / max_exact))
                         * np.float32(half - max_exact)).astype(np.int32)
    large = np.minimum(large, half - 1)
    return bucket + np.where(n < max_exact, n, large)


def _core_inputs(inp, b, flip, S):
    f = np.float32
    c = lambda a: np.ascontiguousarray(a, dtype=f)
    x = inp["x"][b]
    if flip:
        x = x[::-1]
    w_in = inp["w_in"][0]
    o = 0
    cols = {}
    for nm, n in (("q", 1024), ("k", 1024), ("v", 1024), ("z", 2048), ("xbc", 4096), ("dt", 64), ("gate", 2048)):
        cols[nm] = w_in[:, o:o + n]
        o += n
    wdt = cols["dt"]
    dtb = np.concatenate([inp["dt_bias_f"][0], inp["dt_bias_b"][0]])
    alog = np.concatenate([inp["a_log_f"][0], inp["a_log_b"][0]])
    cw = inp["ssm_conv_w"][0]
    z1 = np.zeros((1, XBC), f)
    fw = inp["ffn_conv_w"][0]
    if flip:
        wdt = np.concatenate([wdt[:, 32:], wdt[:, :32]], axis=1)
        dtb = np.concatenate([dtb[32:], dtb[:32]])
        alog = np.concatenate([alog[32:], alog[:32]])
        cw5 = np.concatenate([z1, cw[::-1]], axis=0)
        fw3 = fw[::-1]
    else:
        cw5 = np.concatenate([cw, z1], axis=0)
        fw3 = fw
    sign = -1 if flip else 1
    kl = np.arange(128)[:, None]
    u = np.arange(1152)[None, :] - 512
    bidx = _t5_bucket_np(sign * (kl - u).astype(np.int32))
    rb = inp["rel_bias"]
    toep = np.transpose(rb[bidx], (0, 2, 1))
    lo = rb[_t5_bucket_np(np.array(sign * -1000, np.int32))]
    hi = rb[_t5_bucket_np(np.array(sign * 1000, np.int32))]
    farb = np.broadcast_to(np.stack([lo, hi], axis=1).reshape(1, 16), (128, 16))
    d = {
        "x": c(x),
        "w_q": c(cols["q"]), "w_k": c(cols["k"]), "w_v": c(cols["v"]), "w_z": c(cols["z"]),
        "w_xbc": c(cols["xbc"]), "w_dt": c(wdt), "w_gate": c(cols["gate"]),
        "w_ao": c(inp["w_attn_out"][0]), "w_so": c(inp["w_ssm_out"][0]), "w_o": c(inp["w_out"][0]),
        "w_up": c(inp["w_ffn_up"][0]), "w_dn": c(inp["w_ffn_down"][0]),
        "n1w": c(inp["norm1_w"]), "n2w": c(inp["norm2_w"]),
        "qkw": c(np.stack([np.tile(inp["q_norm_w"][0], 2), np.tile(inp["k_norm_w"][0], 2)], axis=1)),
        "sublnw": c(inp["subln_w"][0].reshape(128, 1)),
        "lamv": c(np.concatenate([inp["lambda_q1"][0], inp["lambda_k1"][0], inp["lambda_q2"][0],
                                  inp["lambda_k2"][0]]).reshape(1, 256)),
        "cw5": c(cw5.T.reshape(32, 128, 5).transpose(1, 0, 2)),
        "cb": c(inp["ssm_conv_b"][0].reshape(32, 128).T),
        "fcw": c(fw3.T.reshape(2 * FC, 128, 3).transpose(1, 0, 2)),
        "fcb": c(inp["ffn_conv_b"][0].reshape(2 * FC, 128).T),
        "dtb": c(dtb.reshape(1, 64)), "alog": c(alog.reshape(1, 64)),
        "dsk": c(inp["d_skip"][0].reshape(1, 32)), "snw": c(inp["ssm_norm_w"][0].reshape(1, SSMI)),
        "toep": c(toep), "farb": c(farb),
    }
    return d


_NC_CACHE = {}


def kernel(**inputs):
    inp = {k: np.asarray(v) for k, v in inputs.items()}
    Bn, S, _ = inp["x"].shape
    debug = bool(inp.pop("_debug", False)) if "_debug" in inp else False
    n = 2 * Bn
    key = (S, debug)
    if key not in _NC_CACHE:
        _NC_CACHE[key] = build_nc(S, lam_init=0.8 - 0.6 * math.exp(0.0), debug=debug)
    nc = _NC_CACHE[key]
    in_maps = [_core_inputs(inp, c // 2, c % 2, S) for c in range(n)]
    res = run_bass_kernel_spmd(nc, in_maps, core_ids=list(range(n)))
    S2 = S // 2
    out = np.empty((Bn, S, D), np.float32)
    for c in range(n):
        o = np.asarray(res.results[c]["out"])
        if c % 2 == 0:
            out[c // 2, :S2] = o
        else:
            out[c // 2, S2:] = o[::-1]
    if debug:
        kernel.last_results = res.results
    return out
```

```python
import math
from contextlib import ExitStack
import numpy as np
import concourse.bass as bass
import concourse.mybir as mybir
from concourse.bass_utils import run_bass_kernel_spmd

F32 = mybir.dt.float32
BF16 = mybir.dt.bfloat16
AF = mybir.ActivationFunctionType
ALU = mybir.AluOpType

NDSEM = 8
D = 1024
KC = 8
H = 8
NH = 32
G = 8
SSMI = 2048
XBC = 4096
FFN = 2816
FC = FFN // 128
EPS = 1e-6


class Buf:
    __slots__ = ("name", "w", "r")

    def __init__(self, name=""):
        self.name = name
        self.w = None
        self.r = []


class Op:
    __slots__ = ("eng", "fn", "deps", "raw", "signal", "sem", "val", "is_dma", "prev_dma", "done")

    def __init__(self, eng, fn, is_dma=False):
        self.eng = eng
        self.fn = fn
        self.deps = set()
        self.raw = set()
        self.signal = False
        self.sem = None
        self.val = 0
        self.is_dma = is_dma
        self.prev_dma = None
        self.done = False


class Prog:
    ENGS = ("pe", "act", "dve", "pool", "sp")

    def __init__(self, nc, sems, dsems, sync_same=False):
        self.nc = nc
        self.sems = sems
        self.dsems = dsems
        self.sync_same = sync_same
        self.q = {e: [] for e in self.ENGS}
        self.dma_hist = {e: [] for e in self.ENGS}
        self.all_dma = []
        self.cnt = {e: 0 for e in self.ENGS}
        self.dcnt = {e: 0 for e in self.ENGS}
        self.last = {e: None for e in self.ENGS}
        self.waited = {e: {} for e in self.ENGS}
        self.nops = 0

    def op(self, eng, fn, reads=(), writes=(), is_dma=False):
        o = Op(eng, fn, is_dma)
        for b in reads:
            if b.w is not None:
                o.deps.add(b.w)
                o.raw.add(b.w)
            b.r.append(o)
        for b in writes:
            if b.w is not None:
                o.deps.add(b.w)
            for r in b.r:
                o.deps.add(r)
            b.w = o
            b.r = []
        o.deps.discard(o)
        if is_dma:
            h = self.dma_hist[eng]
            if len(h) >= NDSEM:
                o.prev_dma = h[-NDSEM]
            h.append(o)
            if len(h) > 4 * NDSEM:
                del h[:2 * NDSEM]
            self.all_dma.append(o)
        self.q[eng].append(o)
        self.nops += 1
        return o

    def dma(self, eng, out, in_, reads=(), writes=()):
        return self.op(eng, lambda e: e.dma_start(out=out, in_=in_), reads, writes, is_dma=True)

    def barrier(self):
        lasts = []
        for e in self.ENGS:
            if self.q[e]:
                lasts.append(self.q[e][-1])
        dmas = list(self.all_dma)
        self.all_dma = []
        for e in self.ENGS:
            o = Op(e, None)
            for l in lasts:
                if l.fn is not None:
                    o.deps.add(l)
            for d in dmas:
                o.deps.add(d)
            self.q[e].append(o)

    def _needs_sync(self, d, ename, israw=True):
        if d.done:
            return False
        if d.is_dma:
            return True
        if d.eng != ename:
            return True
        return self.sync_same and ename != "pe" and israw

    def emit(self, block):
        for e in self.ENGS:
            for o in self.q[e]:
                for d in o.deps:
                    if self._needs_sync(d, o.eng, d in o.raw):
                        d.signal = True
                if o.prev_dma is not None and not o.prev_dma.done:
                    o.prev_dma.signal = True
        for e in self.ENGS:
            for o in self.q[e]:
                if o.sem is not None:
                    continue
                if o.is_dma:
                    i = self.dcnt[e]
                    self.dcnt[e] += 1
                    o.signal = True
                    o.sem = self.dsems[e][i % NDSEM]
                    o.val = 16 * (i // NDSEM + 1)
                elif o.signal:
                    self.cnt[e] += 1
                    o.sem = self.sems[e]
                    o.val = self.cnt[e]

        def run(ename):
            ops = self.q[ename]
            waited = self.waited[ename]

            def body(eng):
                for o in ops:
                    deps = list(o.deps)
                    if o.prev_dma is not None:
                        deps.append(o.prev_dma)
                    need = {}
                    for d in deps:
                        if not self._needs_sync(d, ename, d in o.raw or d is o.prev_dma):
                            continue
                        assert d.sem is not None, "dependency on op emitted later / never signalled"
                        k = d.sem.num
                        if need.get(k, (None, 0))[1] < d.val:
                            need[k] = (d.sem, d.val)
                    for k, (sem, val) in need.items():
                        if waited.get(k, 0) < val:
                            eng.wait_ge(sem, val)
                            waited[k] = val
                    if o.fn is None:
                        continue
                    inst = o.fn(eng)
                    if o.signal:
                        inst.then_inc(o.sem, 16 if o.is_dma else 1)
            return body

        block.tensor(run("pe"))
        block.scalar(run("act"))
        block.vector(run("dve"))
        block.gpsimd(run("pool"))
        block.sync(run("sp"))
        for e in self.ENGS:
            for o in self.q[e]:
                o.done = True
                o.deps = ()
                o.raw = ()
            self.q[e] = []


class Pool:
    def __init__(self, es, nc, name, shape, dt, n, psum=False):
        self.t = []
        for i in range(n):
            if psum:
                h = es.enter_context(nc.psum_tensor(f"{name}{i}", shape, dt))
            else:
                h = es.enter_context(nc.sbuf_tensor(f"{name}{i}", shape, dt))
            self.t.append((h, Buf(f"{name}{i}")))
        self.i = 0

    def next(self):
        r = self.t[self.i % len(self.t)]
        self.i += 1
        return r


def act(P, out, in_, func, reads, writes, **kw):
    return P.op("act", lambda e: e.activation(out=out, in_=in_, func=func, **kw), reads, writes)


def tt(P, eng, out, in0, in1, op, reads, writes):
    return P.op(eng, lambda e: e.tensor_tensor(out=out, in0=in0, in1=in1, op=op), reads, writes)


def stt(P, eng, out, in0, scalar, in1, op0, op1, reads, writes):
    return P.op(eng, lambda e: e.scalar_tensor_tensor(out=out, in0=in0, scalar=scalar, in1=in1, op0=op0, op1=op1),
                reads, writes)


def ts(P, eng, out, in0, s1, s2, op0, op1, reads, writes):
    if s2 is None:
        s2, op1 = 0.0, ALU.add
    return P.op(eng, lambda e: e.tensor_scalar(out=out, in0=in0, scalar1=s1, scalar2=s2, op0=op0, op1=op1),
                reads, writes)


def cp(P, eng, out, in_, reads, writes):
    if eng == "act":
        return P.op("act", lambda e: e.copy(out=out, in_=in_), reads, writes)
    return P.op(eng, lambda e: e.tensor_copy(out=out, in_=in_), reads, writes)


def recip(P, out, in_, reads, writes):
    return P.op("dve", lambda e: e.reciprocal(out=out, in_=in_), reads, writes)


def mmg(P, out, pairs, reads, writes, start=True, stop=True):
    pairs = list(pairs)

    def fn(e):
        n = len(pairs)
        inst = None
        for i, (l, r) in enumerate(pairs):
            inst = e.matmul(out, lhsT=l, rhs=r, start=(start and i == 0), stop=(stop and i == n - 1))
        return inst
    return P.op("pe", fn, reads, writes)


def mms(P, items, reads, writes):
    items = list(items)

    def fn(e):
        inst = None
        for (o, l, r) in items:
            inst = e.matmul(o, lhsT=l, rhs=r, start=True, stop=True)
        return inst
    return P.op("pe", fn, reads, writes)


def trs(P, items, ident, reads, writes):
    items = list(items)

    def fn(e):
        inst = None
        for (o, i) in items:
            inst = e.transpose(out=o, in_=i, identity=ident)
        return inst
    return P.op("pe", fn, reads, writes)


def build_nc(S, lam_init=0.2, debug=False):
    S2 = S // 2
    TQ = S2 + 128
    NT = S // 128
    NTQ = TQ // 128
    nc = bass.Bass("TRN2", target_bir_lowering=False)

    def din(name, shape, dt=F32):
        return nc.dram_tensor(name, list(shape), dt, kind="ExternalInput").ap()

    def dscr(name, shape, dt):
        return nc.dram_tensor(name, list(shape), dt, kind="Internal").ap()

    x_d = din("x", [S, D])
    w_q = din("w_q", [D, 1024])
    w_k = din("w_k", [D, 1024])
    w_v = din("w_v", [D, 1024])
    w_z = din("w_z", [D, SSMI])
    w_xbc = din("w_xbc", [D, XBC])
    w_dt = din("w_dt", [D, 64])
    w_gate = din("w_gate", [D, 2048])
    w_ao = din("w_ao", [1024, D])
    w_so = din("w_so", [SSMI, D])
    w_o = din("w_o", [D, D])
    w_up = din("w_up", [D, 2 * FFN])
    w_dn = din("w_dn", [FFN, D])
    n1w = din("n1w", [1, D])
    n2w = din("n2w", [1, D])
    qkw = din("qkw", [128, 2])
    sublnw = din("sublnw", [128, 1])
    lamv = din("lamv", [1, 256])
    cw5 = din("cw5", [128, 32, 5])
    cb = din("cb", [128, 32])
    fcw = din("fcw", [128, 2 * FC, 3])
    fcb = din("fcb", [128, 2 * FC])
    dtb = din("dtb", [1, 64])
    alog = din("alog", [1, 64])
    dsk = din("dsk", [1, 32])
    snw = din("snw", [1, SSMI])
    toep = din("toep", [128, H, 1152])
    farb = din("farb", [128, 2 * H])
    out_d = nc.dram_tensor("out", [S2, D], F32, kind="ExternalOutput").ap()

    hT_d = dscr("hT_d", [128, KC, S], BF16)
    qT_d = dscr("qT_d", [128, H, TQ], BF16)
    kT_d = dscr("kT_d", [128, H, S], BF16)
    v_d = dscr("v_d", [128, NT, 1024], BF16)
    sz_d = dscr("sz_d", [128, NTQ, SSMI], BF16)
    gT_d = dscr("gT_d", [128, 16, TQ], BF16)
    dt_d = dscr("dt_d", [128, NT, 64], F32)
    raw_d = dscr("raw_d", [128, 32, S + 4], BF16)
    xs_d = dscr("xs_d", [128, NT, SSMI], BF16)
    bt_d = dscr("bt_d", [128, NT, 1024], BF16)
    BT_d = dscr("BT_d", [128, G, S], BF16)
    CT_d = dscr("CT_d", [128, G, S], BF16)
    yb_d = dscr("yb_d", [128, NTQ, SSMI], F32)
    ysT_d = dscr("ysT_d", [128, 16, TQ], BF16)
    oT_d = dscr("oT_d", [128, H, TQ], BF16)
    x1_d = dscr("x1_d", [128, NTQ, D], F32)
    h2T_d = dscr("h2T_d", [128, KC, TQ + 2], BF16)
    wao_b = dscr("wao_b", [128, 8, 1024], BF16)
    wso_b = dscr("wso_b", [128, 16, 1024], BF16)
    wo_b = dscr("wo_b", [128, 8, 1024], BF16)
    wup_b = dscr("wup_b", [128, 8, 2 * FFN], BF16)
    wdn_b = dscr("wdn_b", [128, FC, 1024], BF16)
    dbg = {}
    if debug:
        for nm, shp, dt in (("dbg_hT", [128, KC, S], BF16), ("dbg_qT", [128, H, TQ], BF16),
                            ("dbg_kT", [128, H, S], BF16), ("dbg_v", [128, NT, 1024], BF16),
                            ("dbg_dt", [128, NT, 64], F32), ("dbg_xs", [128, NT, SSMI], BF16),
                            ("dbg_CT", [128, G, S], BF16), ("dbg_bt", [128, NT, 1024], BF16),
                            ("dbg_oT", [128, H, TQ], BF16), ("dbg_ysT", [128, 16, TQ], BF16),
                            ("dbg_yb", [128, NTQ, SSMI], F32), ("dbg_x1", [128, NTQ, D], F32)):
            dbg[nm] = nc.dram_tensor(nm, shp, dt, kind="ExternalOutput").ap()

    with ExitStack() as top:
        sems = {e: top.enter_context(nc.semaphore("s_" + e)) for e in Prog.ENGS}
        dsems = {e: [top.enter_context(nc.semaphore(f"d_{e}{i}")) for i in range(NDSEM)] for e in Prog.ENGS}
        P = Prog(nc, sems, dsems, sync_same=True)

        def sbt(name, shape, dt=F32):
            return top.enter_context(nc.sbuf_tensor(name, shape, dt)), Buf(name)

        ident, b_ident = sbt("ident", [128, 128], BF16)
        identf, b_identf = sbt("identf", [128, 128], F32)
        ones_bf, b_ones = sbt("ones_bf", [128, 128], BF16)
        onesf, b_onesf = sbt("onesf", [128, 128], F32)
        blk64, b_blk = sbt("blk64", [128, 128], BF16)
        mLE, b_mLE = sbt("mLE", [128, 128], F32)
        mGT, b_mGT = sbt("mGT", [128, 128], F32)
        mGE, b_mGE = sbt("mGE", [128, 128], F32)
        mLT, b_mLT = sbt("mLT", [128, 128], F32)
        cst, b_cst = sbt("cst", [128, 8], F32)
        n1bc, b_n1 = sbt("n1bc", [128, D], F32)
        n2bc, b_n2 = sbt("n2bc", [128, D], F32)
        qkw_t, b_qkw = sbt("qkw_t", [128, 2], F32)
        subw_t, b_subw = sbt("subw_t", [128, 1], F32)
        lam_t, b_lam = sbt("lam_t", [128, 256], F32)
        lam_s, b_lams = sbt("lam_s", [128, 4], F32)
        cw5_t, b_cw5 = sbt("cw5_t", [128, 32, 5], F32)
        cb_t, b_cb = sbt("cb_t", [128, 32], F32)
        fcw_t, b_fcw = sbt("fcw_t", [128, 2 * FC, 3], F32)
        fcb_t, b_fcb = sbt("fcb_t", [128, 2 * FC], F32)
        dtb_t, b_dtb = sbt("dtb_t", [128, 64], F32)
        a_t, b_a = sbt("a_t", [128, 64], F32)
        dsk_t, b_dsk = sbt("dsk_t", [128, 32], F32)
        snw_t, b_snw = sbt("snw_t", [128, SSMI], F32)
        farb_t, b_farb = sbt("farb_t", [128, 2 * H], F32)
        zeros_bf, b_zeros = sbt("zeros_bf", [128, 32, 2], BF16)

        with nc.Block() as block:
            P.op("pool", lambda e: e.memset(identf[:], 1.0), writes=[b_identf])
            P.op("pool", lambda e: e.affine_select(out=identf[:], in_=identf[:], pattern=[[-1, 128]],
                                                   compare_op=ALU.is_equal, fill=0.0, base=0, channel_multiplier=1),
                 reads=[b_identf], writes=[b_identf])
            cp(P, "dve", ident[:], identf[:], [b_identf], [b_ident])
            P.op("pool", lambda e: e.memset(ones_bf[:], 1.0), writes=[b_ones])
            P.op("pool", lambda e: e.memset(onesf[:], 1.0), writes=[b_onesf])
            P.op("pool", lambda e: e.memset(zeros_bf[:], 0.0), writes=[b_zeros])
            P.op("pool", lambda e: e.memset(blk64[:], 0.0), writes=[b_blk])
            P.op("pool", lambda e: e.memset(blk64[0:64, 0:64], 1.0), writes=[b_blk])
            P.op("pool", lambda e: e.memset(blk64[64:128, 64:128], 1.0), writes=[b_blk])
            for (m, bm, cop, cm, pat) in ((mLE, b_mLE, ALU.is_ge, -1, 1),
                                          (mGT, b_mGT, ALU.is_gt, 1, -1),
                                          (mGE, b_mGE, ALU.is_ge, 1, -1),
                                          (mLT, b_mLT, ALU.is_gt, -1, 1)):
                P.op("pool", lambda e, m=m: e.memset(m[:], 1.0), writes=[bm])
                P.op("pool", lambda e, m=m, cop=cop, cm=cm, pat=pat: e.affine_select(
                    out=m[:], in_=m[:], pattern=[[pat, 128]], compare_op=cop, fill=0.0, base=0,
                    channel_multiplier=cm), reads=[bm], writes=[bm])
            P.op("pool", lambda e: e.memset(cst[:, 0:1], EPS), writes=[b_cst])
            P.op("pool", lambda e: e.memset(cst[:, 1:2], 1.0), writes=[b_cst])
            P.dma("sp", n1bc[:], n1w[0:1, :].partition_broadcast(128), writes=[b_n1])
            P.dma("sp", n2bc[:], n2w[0:1, :].partition_broadcast(128), writes=[b_n2])
            P.dma("sp", qkw_t[:], qkw[:, :], writes=[b_qkw])
            P.dma("sp", subw_t[:], sublnw[:, :], writes=[b_subw])
            P.dma("sp", lam_t[:], lamv[0:1, :].partition_broadcast(128), writes=[b_lam])
            P.dma("sp", cw5_t[:], cw5[:, :, :], writes=[b_cw5])
            P.dma("sp", cb_t[:], cb[:, :], writes=[b_cb])
            P.dma("sp", fcw_t[:], fcw[:, :, :], writes=[b_fcw])
            P.dma("sp", fcb_t[:], fcb[:, :], writes=[b_fcb])
            P.dma("sp", dtb_t[:], dtb[0:1, :].partition_broadcast(128), writes=[b_dtb])
            P.dma("sp", a_t[:], alog[0:1, :].partition_broadcast(128), writes=[b_a])
            P.dma("sp", dsk_t[:], dsk[0:1, :].partition_broadcast(128), writes=[b_dsk])
            P.dma("sp", snw_t[:], snw[0:1, :].partition_broadcast(128), writes=[b_snw])
            P.dma("sp", farb_t[:], farb[:, :], writes=[b_farb])
            act(P, a_t[:], a_t[:], AF.Exp, [b_a], [b_a])
            ts(P, "dve", a_t[:], a_t[:], -1.0, None, ALU.mult, ALU.bypass, [b_a], [b_a])
            ts(P, "dve", qkw_t[:, 0:1], qkw_t[:, 0:1], 0.125, None, ALU.mult, ALU.bypass, [b_qkw], [b_qkw])
            ts(P, "dve", subw_t[:], subw_t[:], 1.0 - lam_init, None, ALU.mult, ALU.bypass, [b_subw], [b_subw])
            tt(P, "dve", lam_t[:, 0:64], lam_t[:, 0:64], lam_t[:, 64:128], ALU.mult, [b_lam], [b_lam])
            tt(P, "dve", lam_t[:, 128:192], lam_t[:, 128:192], lam_t[:, 192:256], ALU.mult, [b_lam], [b_lam])
            P.op("dve", lambda e: e.reduce_sum(out=lam_s[:, 0:1], in_=lam_t[:, 0:64], axis=mybir.AxisListType.X),
                 [b_lam], [b_lams])
            P.op("dve", lambda e: e.reduce_sum(out=lam_s[:, 1:2], in_=lam_t[:, 128:192], axis=mybir.AxisListType.X),
                 [b_lam], [b_lams])
            act(P, lam_s[:, 2:4], lam_s[:, 0:2], AF.Exp, [b_lams], [b_lams])
            tt(P, "dve", lam_s[:, 0:1], lam_s[:, 3:4], lam_s[:, 2:3], ALU.subtract, [b_lams], [b_lams])
            ts(P, "dve", cst[:, 2:3], lam_s[:, 0:1], -lam_init, None, ALU.add, ALU.bypass, [b_lams], [b_cst])
            P.dma("sp", raw_d[:, :, 0:2], zeros_bf[:], reads=[b_zeros])
            P.dma("sp", raw_d[:, :, S + 2:S + 4], zeros_bf[:], reads=[b_zeros])
            P.dma("sp", h2T_d[:, :, 0:2], zeros_bf[:, 0:KC, 0:2], reads=[b_zeros])
            P.barrier()
            P.emit(block)

        def norm_transpose(P, pl, src_ap, b_src, wbc, b_w, dst_ap, b_dst, evac_eng):
            junk, b_junk = pl["junk"].next()
            st, b_st = pl["stat"].next()
            hb, b_hb = pl["hb"].next()
            pT, b_pT = pl["pT"].next()
            P.op("pool", lambda e: e.memset(st[:, 0:1], 0.0), writes=[b_st])
            act(P, junk[:], src_ap, AF.Square, [b_src], [b_junk, b_st], accum_out=st[:, 0:1])
            act(P, st[:, 1:2], st[:, 0:1], AF.Sqrt, [b_st, b_cst], [b_st], bias=cst[:, 0:1], scale=1.0 / D)
            recip(P, st[:, 2:3], st[:, 1:2], [b_st], [b_st])
            stt(P, "dve", hb[:], src_ap, st[:, 2:3], wbc[:], ALU.mult, ALU.mult, [b_src, b_st, b_w], [b_hb])
            trs(P, [(pT[:, k, :], hb[:, k * 128:(k + 1) * 128]) for k in range(KC)], ident[:],
                [b_hb, b_ident], [b_pT])
            cp(P, evac_eng, dst_ap, pT[:], [b_pT], [b_dst])

        with ExitStack() as es, nc.Block() as block:
            pl = {"junk": Pool(es, nc, "junk", [128, D], F32, 1),
                  "stat": Pool(es, nc, "stat", [128, 4], F32, 4),
                  "hb": Pool(es, nc, "hb", [128, D], BF16, 2),
                  "pT": Pool(es, nc, "pT", [128, KC, 128], BF16, 2, psum=True)}
            xg = Pool(es, nc, "xg", [128, 4, D], F32, 2)
            hst = Pool(es, nc, "hst", [128, KC, 512], BF16, 2)
            xv = x_d.rearrange("(t p) d -> p t d", p=128)
            for gb in range(S // 512):
                xt, b_xt = xg.next()
                P.dma("sp", xt[:], xv[:, gb * 4:(gb + 1) * 4, :], writes=[b_xt])
                hs, b_hs = hst.next()
                for j in range(4):
                    norm_transpose(P, pl, xt[:, j, :], b_xt, n1bc, b_n1, hs[:, :, j * 128:(j + 1) * 128], b_hs,
                                   "act" if j % 2 == 0 else "dve")
                P.dma("pool", hT_d[:, :, gb * 512:(gb + 1) * 512], hs[:], reads=[b_hs])
            P.barrier()
            P.emit(block)

        with ExitStack() as es, nc.Block() as block:
            wst = Pool(es, nc, "wst", [128, KC, 1024], F32, 2)
            wbf = Pool(es, nc, "wbf", [128, KC, 1024], BF16, 2)
            hblk = Pool(es, nc, "hblk", [128, KC, 512], BF16, 3)
            psA = Pool(es, nc, "psA", [128, 512], F32, 6, psum=True)
            psB = Pool(es, nc, "psB", [128, 512], F32, 2, psum=True)
            sqp = Pool(es, nc, "sqp", [128, 512], BF16, 2)
            f32p = Pool(es, nc, "f32p", [128, 512], F32, 4)
            obf = Pool(es, nc, "obf", [128, 512], BF16, 4)
            dtp = Pool(es, nc, "dtp", [128, 64], F32, 4)
            flip = [0]

            def gemm(Wd, N, mode, tok0, tok1, epi):
                Wv = Wd.rearrange("(k p) n -> p k n", p=128)
                for g0 in range(0, N, 1024):
                    gw = min(1024, N - g0)
                    ws, b_ws = wst.next()
                    P.dma("sp", ws[:, :, 0:gw], Wv[:, :, g0:g0 + gw], writes=[b_ws])
                    wb, b_wb = wbf.next()
                    cp(P, "pool", wb[:, 0:4, 0:gw], ws[:, 0:4, 0:gw], [b_ws], [b_wb])
                    cp(P, "pool", wb[:, 4:8, 0:gw], ws[:, 4:8, 0:gw], [b_ws], [b_wb])
                    for tb in range(tok0, tok1, 512):
                        tw = min(512, tok1 - tb)
                        hb, b_hb = hblk.next()
                        P.dma("sp", hb[:, :, 0:tw], hT_d[:, :, tb:tb + tw], writes=[b_hb])
                        if mode == "feat":
                            for jc in range(gw // 128):
                                ps, b_ps = psA.next()
                                mmg(P, ps[:, 0:tw], [(wb[:, k, jc * 128:(jc + 1) * 128], hb[:, k, 0:tw]) for k in range(KC)],
                                    [b_wb, b_hb], [b_ps])
                                epi(g0 // 128 + jc, tb, tw, ps, b_ps)
                        else:
                            for t4 in range(tw // 128):
                                for c0 in range(0, gw, 512):
                                    cw = min(512, gw - c0)
                                    ps, b_ps = psA.next()
                                    mmg(P, ps[:, 0:cw], [(hb[:, k, t4 * 128:(t4 + 1) * 128], wb[:, k, c0:c0 + cw]) for k in range(KC)],
                                        [b_wb, b_hb], [b_ps])
                                    epi(g0 + c0, cw, (tb + t4 * 128) // 128, ps, b_ps)

            def mk_epi_qk(dst, wcol):
                def epi(h, tb, tw, ps, b_ps):
                    sq, b_sq = sqp.next()
                    act(P, sq[:, 0:tw], ps[:, 0:tw], AF.Square, [b_ps], [b_sq])
                    p2, b_p2 = psB.next()
                    mmg(P, p2[:, 0:tw], [(blk64[:], sq[:, 0:tw])], [b_sq, b_blk], [b_p2])
                    sd, b_sd = f32p.next()
                    act(P, sd[:, 0:tw], p2[:, 0:tw], AF.Ln, [b_p2, b_cst], [b_sd], bias=cst[:, 0:1], scale=1.0 / 64)
                    act(P, sd[:, 0:tw], sd[:, 0:tw], AF.Exp, [b_sd], [b_sd], scale=-0.5)
                    ob, b_ob = obf.next()
                    stt(P, "dve", ob[:, 0:tw], ps[:, 0:tw], qkw_t[:, wcol:wcol + 1], sd[:, 0:tw], ALU.mult, ALU.mult,
                        [b_ps, b_qkw, b_sd], [b_ob])
                    P.dma("pool", dst[:, h, tb:tb + tw], ob[:, 0:tw], reads=[b_ob])
                return epi

            def epi_raw(ch, tb, tw, ps, b_ps):
                ob, b_ob = obf.next()
                flip[0] ^= 1
                cp(P, "act" if flip[0] else "dve", ob[:, 0:tw], ps[:, 0:tw], [b_ps], [b_ob])
                P.dma("pool", raw_d[:, ch, 2 + tb:2 + tb + tw], ob[:, 0:tw], reads=[b_ob])

            def epi_gate(ch, tb, tw, ps, b_ps):
                ob, b_ob = obf.next()
                act(P, ob[:, 0:tw], ps[:, 0:tw], AF.Sigmoid, [b_ps], [b_ob])
                P.dma("pool", gT_d[:, ch, tb:tb + tw], ob[:, 0:tw], reads=[b_ob])

            def epi_v(c0, cw, t, ps, b_ps):
                ob, b_ob = obf.next()
                flip[0] ^= 1
                cp(P, "act" if flip[0] else "dve", ob[:, 0:cw], ps[:, 0:cw], [b_ps], [b_ob])
                P.dma("pool", v_d[:, t, c0:c0 + cw], ob[:, 0:cw], reads=[b_ob])

            def epi_z(c0, cw, t, ps, b_ps):
                ob, b_ob = obf.next()
                act(P, ob[:, 0:cw], ps[:, 0:cw], AF.Silu, [b_ps], [b_ob])
                P.dma("pool", sz_d[:, t, c0:c0 + cw], ob[:, 0:cw], reads=[b_ob])

            def epi_dt(c0, cw, t, ps, b_ps):
                d1, b_d1 = dtp.next()
                tt(P, "dve", d1[:], ps[:, 0:64], dtb_t[:], ALU.add, [b_ps, b_dtb], [b_d1])
                act(P, d1[:], d1[:], AF.Exp, [b_d1], [b_d1])
                act(P, d1[:], d1[:], AF.Ln, [b_d1, b_cst], [b_d1], bias=cst[:, 1:2], scale=1.0)
                P.dma("pool", dt_d[:, t, :], d1[:], reads=[b_d1])

            gemm(w_q, 1024, "feat", 0, TQ, mk_epi_qk(qT_d, 0))
            gemm(w_k, 1024, "feat", 0, S, mk_epi_qk(kT_d, 1))
            gemm(w_dt, 64, "tok", 0, S, epi_dt)
            gemm(w_v, 1024, "tok", 0, S, epi_v)
            gemm(w_gate, 2048, "feat", 0, TQ, epi_gate)
            gemm(w_z, SSMI, "tok", 0, TQ, epi_z)
            gemm(w_xbc, XBC, "feat", 0, S, epi_raw)
            P.barrier()
            P.emit(block)

        with ExitStack() as es, nc.Block() as block:
            rawp = Pool(es, nc, "rawp", [128, 32, 516], BF16, 2)
            csp = Pool(es, nc, "csp", [128, 512], BF16, 3)
            stok = Pool(es, nc, "stok", [128, 4, 3072], BF16, 1)
            sfeat = Pool(es, nc, "sfeat", [128, 16, 512], BF16, 2)
            pTp = Pool(es, nc, "pTc", [128, 4, 128], BF16, 3, psum=True)
            pcv = Pool(es, nc, "pcv", [128, 512], F32, 5, psum=True)
            dg = es.enter_context(nc.sbuf_tensor("dg", [128, 32, 5, 128], BF16))
            b_dg = Buf()
            for ch in range(32):
                tt(P, "dve", dg[:, ch, :, :], ident[:].unsqueeze(1).to_broadcast([128, 5, 128]),
                   cw5_t[:, ch, :].unsqueeze(2).to_broadcast([128, 5, 128]), ALU.mult, [b_ident, b_cw5], [b_dg])
            k = 0
            for tb in range(0, S, 512):
                rw, b_rw = rawp.next()
                for c8 in range(0, 32, 8):
                    P.dma("sp", rw[:, c8:c8 + 8, :], raw_d[:, c8:c8 + 8, tb:tb + 516], writes=[b_rw])
                stk, b_stk = stok.next()
                sft, b_sft = sfeat.next()
                for ch in range(32):
                    pc, b_pc = pcv.next()
                    mmg(P, pc[:], [(dg[:, ch, o, :], rw[:, ch, o:o + 512]) for o in range(5)], [b_dg, b_rw], [b_pc])
                    if ch < 24:
                        cs, b_cs = csp.next()
                        act(P, cs[:], pc[:], AF.Silu, [b_pc, b_cb], [b_cs], bias=cb_t[:, ch:ch + 1], scale=1.0)
                        pT, b_pT = pTp.next()
                        trs(P, [(pT[:, j, :], cs[:, j * 128:(j + 1) * 128]) for j in range(4)], ident[:],
                            [b_cs, b_ident], [b_pT])
                        cp(P, "dve", stk[:, :, ch * 128:(ch + 1) * 128], pT[:], [b_pT], [b_stk])
                        if ch >= 16:
                            cp(P, "pool", sft[:, ch - 16, :], cs[:], [b_cs], [b_sft])
                    else:
                        act(P, sft[:, ch - 16, :], pc[:], AF.Silu, [b_pc, b_cb], [b_sft], bias=cb_t[:, ch:ch + 1], scale=1.0)
                t0 = tb // 128
                P.dma("pool", xs_d[:, t0:t0 + 4, :], stk[:, :, 0:2048], reads=[b_stk])
                P.dma("pool", bt_d[:, t0:t0 + 4, :], stk[:, :, 2048:3072], reads=[b_stk])
                P.dma("pool", BT_d[:, :, tb:tb + 512], sft[:, 0:8, :], reads=[b_sft])
                P.dma("pool", CT_d[:, :, tb:tb + 512], sft[:, 8:16, :], reads=[b_sft])
            P.barrier()
            P.emit(block)

        with ExitStack() as es, nc.Block() as block:
            kTp = Pool(es, nc, "kTp", [128, S], BF16, 2)
            qTp = Pool(es, nc, "qTp", [128, TQ], BF16, 2)
            vp = Pool(es, nc, "vp", [128, NT, 128], BF16, 2)
            tpp = Pool(es, nc, "tpp", [128, 1152], F32, 2)
            psS = Pool(es, nc, "psS", [128, 2, 512], F32, 2, psum=True)
            psO = Pool(es, nc, "psO", [128, 2, 512], F32, 1, psum=True)
            psZ1 = Pool(es, nc, "psZ1", [128, 512], F32, 1, psum=True)
            psZ0 = Pool(es, nc, "psZ0", [128, 512], F32, 1, psum=True)
            ptp = Pool(es, nc, "ptp", [128, 2, 512], BF16, 8)
            ndp = Pool(es, nc, "ndp", [128, 2, 512], F32, 2)
            zap = [Pool(es, nc, f"zap{par}", [128, 512], F32, 1) for par in range(2)]
            e32 = Pool(es, nc, "e32", [128, 512], F32, 8)
            ebf = Pool(es, nc, "ebf", [128, 512], BF16, 4)
            qblocks = [(q0, min(512, TQ - q0)) for q0 in range(0, TQ, 512)]
            wcs = Pool(es, nc, "wcs", [128, 8, 512], F32, 1)
            wcb = Pool(es, nc, "wcb", [128, 8, 512], BF16, 1)
            slabs = []
            for (Wd, dstd, kk_tot, ncols) in ((w_ao, wao_b, 8, 1024), (w_so, wso_b, 16, 1024), (w_o, wo_b, 8, 1024),
                                               (w_up, wup_b, 8, 2 * FFN), (w_dn, wdn_b, FC, 1024)):
                Wv_ = Wd.rearrange("(k p) n -> p k n", p=128)
                for k0 in range(0, kk_tot, 8):
                    kk = min(8, kk_tot - k0)
                    for n0 in range(0, ncols, 512):
                        slabs.append((Wv_[:, k0:k0 + kk, n0:n0 + 512], dstd[:, k0:k0 + kk, n0:n0 + 512], kk))

            def cast_slab():
                if not slabs:
                    return
                src, dst, kk = slabs.pop(0)
                ws_, b_ws_ = wcs.next()
                P.dma("sp", ws_[:, 0:kk, :], src, writes=[b_ws_])
                wb_, b_wb_ = wcb.next()
                cp(P, "pool", wb_[:, 0:kk, :], ws_[:, 0:kk, :], [b_ws_], [b_wb_])
                P.dma("pool", dst, wb_[:, 0:kk, :], reads=[b_wb_])
            osp = Pool(es, nc, "osp", [128, 2, 512], F32, 2)
            z1sp = Pool(es, nc, "z1sp", [128, 512], F32, 2)
            DEFER = 8
            pend = [None, None, None]

            def classify(kt, q0, qw):
                d = kt * 128 - q0
                if d - (qw - 1) >= 91:
                    return "hi"
                if d + 127 <= -91:
                    return "lo"
                return "near"

            for h in range(H):
                kT, b_kT = kTp.next()
                P.dma("sp", kT[:], kT_d[:, h, :], writes=[b_kT])
                qT, b_qT = qTp.next()
                P.dma("sp", qT[:], qT_d[:, h, :], writes=[b_qT])
                vt, b_vt = vp.next()
                for t16 in range(0, NT, 16):
                    t1 = min(NT, t16 + 16)
                    P.dma("sp", vt[:, t16:t1, :], v_d[:, t16:t1, h * 128:(h + 1) * 128], writes=[b_vt])
                tp, b_tp = tpp.next()
                P.dma("sp", tp[:], toep[:, h, :], writes=[b_tp])
                for (q0, qw) in qblocks:
                    pO, b_pO = psO.next()
                    za = [zap[par].next() for par in range(2)]
                    pZ1, b_Z1 = psZ1.next()
                    sq_ = {}
                    nd_ = {}

                    def qk(kt):
                        pS, b_pS = psS.next()
                        mms(P, [(pS[:, m, 0:qw], kT[m * 64:(m + 1) * 64, kt * 128:(kt + 1) * 128],
                                 qT[m * 64:(m + 1) * 64, q0:q0 + qw]) for m in range(2)], [b_kT, b_qT], [b_pS])
                        sq_[kt] = (pS, b_pS)
                        if classify(kt, q0, qw) == "near":
                            nd, b_nd = ndp.next()
                            j0 = 512 - (kt * 128 - q0)
                            tt(P, "dve", nd[:, :, 0:qw], pS[:, :, 0:qw],
                               tp[:, j0:j0 + qw].unsqueeze(1).to_broadcast([128, 2, qw]), ALU.add, [b_pS, b_tp], [b_nd])
                            nd_[kt] = (nd, b_nd)
                    qk(0)
                    for kt in range(NT):
                        if kt + 1 < NT:
                            qk(kt + 1)
                        pS, b_pS = sq_.pop(kt)
                        pt, b_pt = ptp.next()
                        cls = classify(kt, q0, qw)
                        if cls == "hi":
                            act(P, pt[:, :, 0:qw], pS[:, :, 0:qw], AF.Exp, [b_pS, b_farb], [b_pt],
                                bias=farb_t[:, 2 * h + 1:2 * h + 2], scale=1.0)
                        elif cls == "lo":
                            act(P, pt[:, :, 0:qw], pS[:, :, 0:qw], AF.Exp, [b_pS, b_farb], [b_pt],
                                bias=farb_t[:, 2 * h:2 * h + 1], scale=1.0)
                        else:
                            nd, b_nd = nd_.pop(kt)
                            act(P, pt[:, :, 0:qw], nd[:, :, 0:qw], AF.Exp, [b_nd], [b_pt])
                        for (stage, at) in ((0, 0), (1, min(DEFER, NT - 1)), (2, min(2 * DEFER, NT - 1))):
                            if kt == at and pend[stage] is not None:
                                pend[stage]()
                                pend[stage] = None
                        pv_items = [(pO[:, m, 0:qw], vt[:, kt, :], pt[:, m, 0:qw]) for m in range(2)]

                        def pvfn(e, items=pv_items, st=(kt == 0), sp_=(kt == NT - 1)):
                            inst = None
                            for (o_, l_, r_) in items:
                                inst = e.matmul(o_, lhsT=l_, rhs=r_, start=st, stop=sp_)
                            return inst
                        P.op("pe", pvfn, [b_vt, b_pt], [b_pO])
                        mmg(P, pZ1[:, 0:qw], [(ones_bf[:], pt[:, 1, 0:qw])], [b_ones, b_pt], [b_Z1],
                            start=(kt == 0), stop=(kt == NT - 1))
                        zt, b_zt = za[kt % 2]
                        if kt < 2:
                            cp(P, "dve", zt[:, 0:qw], pt[:, 0, 0:qw], [b_pt], [b_zt])
                        else:
                            tt(P, "dve", zt[:, 0:qw], zt[:, 0:qw], pt[:, 0, 0:qw], ALU.add, [b_zt, b_pt], [b_zt])
                    def fast(qw=qw, pO=pO, b_pO=b_pO, pZ1=pZ1, b_Z1=b_Z1, za=za, st_=None):
                        oS, b_oS = osp.next()
                        cp(P, "act", oS[:, :, 0:qw], pO[:, :, 0:qw], [b_pO], [b_oS])
                        z1s, b_z1s = z1sp.next()
                        cp(P, "act", z1s[:, 0:qw], pZ1[:, 0:qw], [b_Z1], [b_z1s])
                        (z0, b_z0), (z1, b_z1) = za
                        if NT > 1:
                            tt(P, "dve", z0[:, 0:qw], z0[:, 0:qw], z1[:, 0:qw], ALU.add, [b_z0, b_z1], [b_z0])
                        pZ0, b_Z0 = psZ0.next()
                        mmg(P, pZ0[:, 0:qw], [(onesf[:], z0[:, 0:qw])], [b_onesf, b_z0], [b_Z0])
                        return oS, b_oS, z1s, b_z1s, pZ0, b_Z0
                    ctx = {}

                    def stage0(ctx=ctx, fast=fast):
                        ctx["f"] = fast()

                    def stage1(ctx=ctx, qw=qw):
                        oS, b_oS, z1s, b_z1s, pZ0, b_Z0 = ctx["f"]
                        r0, b_r0 = e32.next()
                        r1, b_r1 = e32.next()
                        act(P, r0[:, 0:qw], pZ0[:, 0:qw], AF.Ln, [b_Z0], [b_r0])
                        act(P, r1[:, 0:qw], z1s[:, 0:qw], AF.Ln, [b_z1s], [b_r1])
                        act(P, r0[:, 0:qw], r0[:, 0:qw], AF.Exp, [b_r0], [b_r0], scale=-1.0)
                        act(P, r1[:, 0:qw], r1[:, 0:qw], AF.Exp, [b_r1], [b_r1], scale=-1.0)
                        tt(P, "pool", oS[:, 0, 0:qw], oS[:, 0, 0:qw], r0[:, 0:qw], ALU.mult, [b_oS, b_r0], [b_oS])
                        tt(P, "pool", oS[:, 1, 0:qw], oS[:, 1, 0:qw], r1[:, 0:qw], ALU.mult, [b_oS, b_r1], [b_oS])
                        o32, b_o32 = e32.next()
                        stt(P, "dve", o32[:, 0:qw], oS[:, 1, 0:qw], cst[:, 2:3], oS[:, 0, 0:qw], ALU.mult, ALU.add,
                            [b_oS, b_cst], [b_o32])
                        sq, b_sq = ebf.next()
                        tt(P, "pool", sq[:, 0:qw], o32[:, 0:qw], o32[:, 0:qw], ALU.mult, [b_o32], [b_sq])
                        mmg(P, pZ0[:, 0:qw], [(ones_bf[:], sq[:, 0:qw])], [b_ones, b_sq], [b_Z0])
                        ctx["o32"] = (o32, b_o32)

                    def stage2(ctx=ctx, h=h, q0=q0, qw=qw):
                        oS, b_oS, z1s, b_z1s, pZ0, b_Z0 = ctx["f"]
                        o32, b_o32 = ctx["o32"]
                        sd, b_sd = e32.next()
                        act(P, sd[:, 0:qw], pZ0[:, 0:qw], AF.Ln, [b_Z0, b_cst], [b_sd], bias=cst[:, 0:1], scale=1.0 / 128)
                        act(P, sd[:, 0:qw], sd[:, 0:qw], AF.Exp, [b_sd], [b_sd], scale=-0.5)
                        ob, b_ob = ebf.next()
                        stt(P, "dve", ob[:, 0:qw], o32[:, 0:qw], subw_t[:, 0:1], sd[:, 0:qw], ALU.mult, ALU.mult,
                            [b_o32, b_subw, b_sd], [b_ob])
                        P.dma("pool", oT_d[:, h, q0:q0 + qw], ob[:, 0:qw], reads=[b_ob])
                    for st_i in range(3):
                        if pend[st_i] is not None:
                            pend[st_i]()
                            pend[st_i] = None
                    pend[0], pend[1], pend[2] = stage0, stage1, stage2
                    cast_slab()
            for st_i in range(3):
                if pend[st_i] is not None:
                    pend[st_i]()
                    pend[st_i] = None
            while slabs:
                cast_slab()
            P.barrier()
            P.emit(block)

        with ExitStack() as es, nc.Block() as block:
            xsp = Pool(es, nc, "xsp", [128, SSMI], BF16, 2)
            btp = Pool(es, nc, "btp", [128, 1024], BF16, 2)
            BTp = Pool(es, nc, "BTp", [128, G, 128], BF16, 2)
            CTp = Pool(es, nc, "CTp", [128, G, 128], BF16, 2)
            dtl = Pool(es, nc, "dtl", [128, 64], F32, 2)
            szp = Pool(es, nc, "szp", [128, SSMI], BF16, 2)
            ybp = Pool(es, nc, "ybp", [128, SSMI], F32, 2)
            dap = Pool(es, nc, "dap", [128, 32], F32, 2)
            exp_ = Pool(es, nc, "exp_", [128, 96], F32, 2)
            w2p = Pool(es, nc, "w2p", [128, 32], F32, 2)
            xdtp = Pool(es, nc, "xdtp", [128, SSMI], BF16, 2)
            xwp = Pool(es, nc, "xwp", [128, SSMI], BF16, 2)
            ychp = Pool(es, nc, "ychp", [128, SSMI], F32, 2)
            cbmp = Pool(es, nc, "cbmp", [128, 128], F32, 9)
            sglp = Pool(es, nc, "sglp", [128, 4, 128], F32, 3)
            decp = Pool(es, nc, "decp", [128, 4, 128], F32, 9)
            MTp = Pool(es, nc, "MTp", [128, 4, 128], BF16, 9)
            tmpp = Pool(es, nc, "tmpp", [128, 256], F32, 4)
            state, b_state = [None, None], [None, None]
            sbf, b_sbf = [None, None], [None, None]
            for dd in range(2):
                state[dd] = es.enter_context(nc.sbuf_tensor(f"state{dd}", [128, G, 256], F32))
                b_state[dd] = [Buf() for _ in range(G)]
                sbf[dd] = es.enter_context(nc.sbuf_tensor(f"sbf{dd}", [128, G, 256], BF16))
                b_sbf[dd] = Buf()
            ps_sm = Pool(es, nc, "ps_sm", [128, 96], F32, 1, psum=True)
            ps_cb = Pool(es, nc, "ps_cb", [128, 128], F32, 1, psum=True)
            ps_sg = Pool(es, nc, "ps_sg", [128, 4, 128], F32, 2, psum=True)
            ps_y = Pool(es, nc, "ps_y", [128, 2, 256], F32, 2, psum=True)
            ps_ds = Pool(es, nc, "ps_ds", [128, 256], F32, 1, psum=True)
            ps_tr = Pool(es, nc, "ps_tr", [128, 4, 128], BF16, 1, psum=True)
            ysn = Pool(es, nc, "ysn", [128, SSMI], BF16, 2)
            ystg = Pool(es, nc, "ystg", [128, 16, 128], BF16, 2)
            stg = Pool(es, nc, "stg", [128, 16], F32, 2)
            junk2 = Pool(es, nc, "junk2", [128, 256], F32, 1)

            for dd in range(2):
                P.op("pool", lambda e, dd=dd: e.memset(state[dd][:], 0.0), writes=b_state[dd])
                P.op("pool", lambda e, dd=dd: e.memset(sbf[dd][:], 0.0), writes=[b_sbf[dd]])

            def ssd_chunk(c, dirn, full):
                mA, b_mA = (mLE, b_mLE) if dirn == 0 else (mGE, b_mGE)
                mB, b_mB = (mGT, b_mGT) if dirn == 0 else (mLT, b_mLT)
                st, b_st, sb_, b_sb = state[dirn], b_state[dirn], sbf[dirn], b_sbf[dirn]
                xs, b_xs = xsp.next()
                P.dma("sp", xs[:], xs_d[:, c, :], writes=[b_xs])
                bt, b_bt = btp.next()
                P.dma("sp", bt[:], bt_d[:, c, :], writes=[b_bt])
                dtt, b_dtt = dtl.next()
                P.dma("sp", dtt[:], dt_d[:, c, :], writes=[b_dtt])
                if full:
                    BT, b_BT = BTp.next()
                    P.dma("sp", BT[:], BT_d[:, :, c * 128:(c + 1) * 128], writes=[b_BT])
                    CT, b_CT = CTp.next()
                    P.dma("sp", CT[:], CT_d[:, :, c * 128:(c + 1) * 128], writes=[b_CT])
                dts = dtt[:, dirn * 32:(dirn + 1) * 32]
                da, b_da = dap.next()
                tt(P, "dve", da[:], dts, a_t[:, dirn * 32:(dirn + 1) * 32], ALU.mult, [b_dtt, b_a], [b_da])
                psm, b_psm = ps_sm.next()
                mms(P, [(psm[:, 0:32], mA[:], da[:]), (psm[:, 32:64], mB[:], da[:]), (psm[:, 64:96], onesf[:], da[:])],
                    [b_mA, b_mB, b_onesf, b_da], [b_psm])
                ex, b_ex = exp_.next()
                act(P, ex[:], psm[:], AF.Exp, [b_psm], [b_ex])
                w2, b_w2 = w2p.next()
                tt(P, "dve", w2[:], dts, ex[:, 32:64], ALU.mult, [b_dtt, b_ex], [b_w2])
                xw, b_xw = xwp.next()
                tt(P, "pool", xw[:].rearrange("p (r j) -> p r j", r=NH), xs[:].rearrange("p (r j) -> p r j", r=NH),
                   w2[:].unsqueeze(2).to_broadcast([128, NH, 64]), ALU.mult, [b_xs, b_w2], [b_xw])
                if full:
                    xdt, b_xdt = xdtp.next()
                    tt(P, "dve", xdt[:].rearrange("p (r j) -> p r j", r=NH), xs[:].rearrange("p (r j) -> p r j", r=NH),
                       dts.unsqueeze(2).to_broadcast([128, NH, 64]), ALU.mult, [b_xs, b_dtt], [b_xdt])
                    ych, b_ych = ychp.next()
                if full:
                    stA = []
                    for g in range(G):
                        pcb, b_pcb = ps_cb.next()
                        mmg(P, pcb[:], [(BT[:, g, :], CT[:, g, :])], [b_BT, b_CT], [b_pcb])
                        cbm, b_cbm = cbmp.next()
                        tt(P, "dve", cbm[:], pcb[:], mA[:], ALU.mult, [b_pcb, b_mA], [b_cbm])
                        sgl, b_sgl = sglp.next()
                        tt(P, "pool", sgl[:], mB[:].unsqueeze(1).to_broadcast([128, 4, 128]),
                           da[:, g * 4:(g + 1) * 4].unsqueeze(2).to_broadcast([128, 4, 128]), ALU.mult,
                           [b_mB, b_da], [b_sgl])
                        psg, b_psg = ps_sg.next()
                        mms(P, [(psg[:, r, :], sgl[:, r, :], mA[:]) for r in range(4)], [b_sgl, b_mA], [b_psg])
                        dec, b_dec = decp.next()
                        act(P, dec[:], psg[:], AF.Exp, [b_psg], [b_dec])
                        stA.append((cbm, b_cbm, dec, b_dec))
                    stB = []
                    for g in range(G):
                        cbm, b_cbm, dec, b_dec = stA[g]
                        MT, b_MT = MTp.next()
                        tt(P, "dve", MT[:], dec[:], cbm[:].unsqueeze(1).to_broadcast([128, 4, 128]), ALU.mult,
                           [b_dec, b_cbm], [b_MT])
                        stB.append((MT, b_MT))
                    for g in range(G):
                        MT, b_MT = stB[g]
                        py, b_py = ps_y.next()
                        mms(P, [(py[:, 0, r * 64:(r + 1) * 64], MT[:, r, :], xdt[:, (g * 4 + r) * 64:(g * 4 + r + 1) * 64])
                                for r in range(4)] + [(py[:, 1, :], CT[:, g, :], sb_[:, g, :])],
                            [b_MT, b_xdt, b_CT, b_sb], [b_py])
                        tmp, b_tmp = tmpp.next()
                        tt(P, "dve", tmp[:].rearrange("p (r j) -> p r j", r=4), py[:, 1, :].rearrange("p (r j) -> p r j", r=4),
                           ex[:, g * 4:(g + 1) * 4].unsqueeze(2).to_broadcast([128, 4, 64]), ALU.mult,
                           [b_py, b_ex], [b_tmp])
                        tt(P, "dve", ych[:, g * 256:(g + 1) * 256], py[:, 0, :], tmp[:], ALU.add, [b_py, b_tmp], [b_ych])
                for g in range(G):
                    pds, b_pds = ps_ds.next()
                    mmg(P, pds[:], [(bt[:, g * 128:(g + 1) * 128], xw[:, g * 256:(g + 1) * 256])], [b_bt, b_xw], [b_pds])
                    tt(P, "pool", st[:, g, :].rearrange("p (r j) -> p r j", r=4), st[:, g, :].rearrange("p (r j) -> p r j", r=4),
                       ex[:, 64 + g * 4:64 + (g + 1) * 4].unsqueeze(2).to_broadcast([128, 4, 64]), ALU.mult,
                       [b_st[g], b_ex], [b_st[g]])
                    tt(P, "dve", st[:, g, :], st[:, g, :], pds[:], ALU.add, [b_st[g], b_pds], [b_st[g]])
                    cp(P, "act", sb_[:, g, :], st[:, g, :], [b_st[g]], [b_sb])
                if full:
                    return ych, b_ych, xs, b_xs
                return None

            for c in range(NT - 1, NTQ - 1, -1):
                ssd_chunk(c, 1, False)
            b_ybd = [Buf() for _ in range(NTQ)]
            for c in range(NTQ - 1, -1, -1):
                ych, b_ych, xs, b_xs = ssd_chunk(c, 1, True)
                P.dma("pool", yb_d[:, c, :], ych[:], reads=[b_ych], writes=[b_ybd[c]])
            for c in range(NTQ):
                ych, b_ych, xs, b_xs = ssd_chunk(c, 0, True)
                yb, b_yb = ybp.next()
                P.dma("sp", yb[:], yb_d[:, c, :], reads=[b_ybd[c]], writes=[b_yb])
                sz, b_sz = szp.next()
                P.dma("sp", sz[:], sz_d[:, c, :], writes=[b_sz])
                tt(P, "dve", ych[:], ych[:], yb[:], ALU.add, [b_ych, b_yb], [b_ych])
                tt(P, "pool", yb[:].rearrange("p (r j) -> p r j", r=NH), xs[:].rearrange("p (r j) -> p r j", r=NH),
                   dsk_t[:].unsqueeze(2).to_broadcast([128, NH, 64]), ALU.mult, [b_xs, b_dsk], [b_yb])
                tt(P, "dve", ych[:], ych[:], yb[:], ALU.add, [b_ych, b_yb], [b_ych])
                tt(P, "pool", ych[:], ych[:], sz[:], ALU.mult, [b_ych, b_sz], [b_ych])
                sg, b_sg = stg.next()
                jk, b_jk = junk2.next()
                P.op("pool", lambda e, sg=sg: e.memset(sg[:, 0:8], 0.0), writes=[b_sg])
                for g in range(G):
                    act(P, jk[:], ych[:, g * 256:(g + 1) * 256], AF.Square, [b_ych], [b_jk, b_sg], accum_out=sg[:, g:g + 1])
                act(P, sg[:, 8:16], sg[:, 0:8], AF.Sqrt, [b_sg, b_cst], [b_sg], bias=cst[:, 0:1], scale=1.0 / 256)
                recip(P, sg[:, 8:16], sg[:, 8:16], [b_sg], [b_sg])
                tt(P, "dve", ych[:].rearrange("p (g j) -> p g j", g=G), ych[:].rearrange("p (g j) -> p g j", g=G),
                   sg[:, 8:16].unsqueeze(2).to_broadcast([128, G, 256]), ALU.mult, [b_ych, b_sg], [b_ych])
                yn, b_yn = ysn.next()
                tt(P, "pool", yn[:], ych[:], snw_t[:], ALU.mult, [b_ych, b_snw], [b_yn])
                ys_, b_ys = ystg.next()
                for q4 in range(4):
                    ptr, b_ptr = ps_tr.next()
                    trs(P, [(ptr[:, j, :], yn[:, (q4 * 4 + j) * 128:(q4 * 4 + j + 1) * 128]) for j in range(4)], ident[:],
                        [b_yn, b_ident], [b_ptr])
                    cp(P, "act", ys_[:, q4 * 4:(q4 + 1) * 4, :], ptr[:], [b_ptr], [b_ys])
                P.dma("pool", ysT_d[:, :, c * 128:(c + 1) * 128], ys_[:], reads=[b_ys])
            P.barrier()
            P.emit(block)

        with ExitStack() as es:
          wao, b_wao = es.enter_context(nc.sbuf_tensor("wao", [128, 8, 1024], BF16)), Buf()
          wso, b_wso = es.enter_context(nc.sbuf_tensor("wso", [128, 16, 1024], BF16)), Buf()
          wo, b_wo = es.enter_context(nc.sbuf_tensor("wo", [128, 8, 1024], BF16)), Buf()
          with nc.Block() as block:
            P.dma("sp", wao[:], wao_b[:, :, :], writes=[b_wao])
            for k0 in range(0, 16, 8):
                P.dma("sp", wso[:, k0:k0 + 8, :], wso_b[:, k0:k0 + 8, :], writes=[b_wso])
            P.dma("sp", wo[:], wo_b[:, :, :], writes=[b_wo])
            oTp = Pool(es, nc, "oTp", [128, 8, 512], BF16, 2)
            ysTp = Pool(es, nc, "ysTp", [128, 16, 512], BF16, 1)
            gTp = Pool(es, nc, "gTp", [128, 16, 512], BF16, 1)
            xrp = Pool(es, nc, "xrp", [128, 4, D], F32, 1)
            mixp = Pool(es, nc, "mixp", [128, 8, 512], BF16, 1)
            t32 = Pool(es, nc, "t32", [128, 512], F32, 4)
            x1p = Pool(es, nc, "x1p", [128, D], F32, 3)
            h2st = Pool(es, nc, "h2st", [128, KC, 512], BF16, 2)
            psA = Pool(es, nc, "ps4A", [128, 512], F32, 2, psum=True)
            psS_ = Pool(es, nc, "ps4S", [128, 512], F32, 2, psum=True)
            psX = Pool(es, nc, "ps4X", [128, 2, 512], F32, 1, psum=True)
            pl = {"junk": Pool(es, nc, "junk4", [128, D], F32, 1),
                  "stat": Pool(es, nc, "stat4", [128, 4], F32, 4),
                  "hb": Pool(es, nc, "hb4", [128, D], BF16, 2),
                  "pT": Pool(es, nc, "pT4", [128, KC, 128], BF16, 2, psum=True)}
            xv = x_d.rearrange("(t p) d -> p t d", p=128)
            for tb in range(0, TQ, 512):
                tw = min(512, TQ - tb)
                nt4 = tw // 128
                oT, b_oT = oTp.next()
                P.dma("sp", oT[:, :, 0:tw], oT_d[:, :, tb:tb + tw], writes=[b_oT])
                ysT, b_ysT = ysTp.next()
                P.dma("sp", ysT[:, :, 0:tw], ysT_d[:, :, tb:tb + tw], writes=[b_ysT])
                gT, b_gT = gTp.next()
                P.dma("sp", gT[:, :, 0:tw], gT_d[:, :, tb:tb + tw], writes=[b_gT])
                xr, b_xr = xrp.next()
                P.dma("sp", xr[:, 0:nt4, :], xv[:, tb // 128:tb // 128 + nt4, :], writes=[b_xr])
                mix, b_mix = mixp.next()
                for n in range(8):
                    pa, b_pa = psA.next()
                    mmg(P, pa[:, 0:tw], [(wao[:, k, n * 128:(n + 1) * 128], oT[:, k, 0:tw]) for k in range(8)],
                        [b_wao, b_oT], [b_pa])
                    pS, b_pS = psS_.next()
                    mmg(P, pS[:, 0:tw], [(wso[:, k, n * 128:(n + 1) * 128], ysT[:, k, 0:tw]) for k in range(16)],
                        [b_wso, b_ysT], [b_pS])
                    ta, b_ta = t32.next()
                    tt(P, "dve", ta[:, 0:tw], pa[:, 0:tw], gT[:, n, 0:tw], ALU.mult, [b_pa, b_gT], [b_ta])
                    tb_, b_tb = t32.next()
                    tt(P, "dve", tb_[:, 0:tw], pS[:, 0:tw], gT[:, 8 + n, 0:tw], ALU.mult, [b_pS, b_gT], [b_tb])
                    tt(P, "pool", mix[:, n, 0:tw], ta[:, 0:tw], tb_[:, 0:tw], ALU.add, [b_ta, b_tb], [b_mix])
                h2s, b_h2s = h2st.next()
                for t4 in range(nt4):
                    px, b_px = psX.next()
                    for hf in range(2):
                        mmg(P, px[:, hf, :], [(mix[:, k, t4 * 128:(t4 + 1) * 128], wo[:, k, hf * 512:(hf + 1) * 512])
                                              for k in range(8)], [b_mix, b_wo], [b_px])
                    x1, b_x1 = x1p.next()
                    tt(P, "dve", x1[:], px[:].rearrange("p a b -> p (a b)"), xr[:, t4, :], ALU.add, [b_px, b_xr], [b_x1])
                    P.dma("pool", x1_d[:, tb // 128 + t4, :], x1[:], reads=[b_x1])
                    norm_transpose(P, pl, x1[:], b_x1, n2bc, b_n2, h2s[:, :, t4 * 128:(t4 + 1) * 128], b_h2s,
                                   "act" if t4 % 2 == 0 else "dve")
                P.dma("pool", h2T_d[:, :, 1 + tb:1 + tb + tw], h2s[:, :, 0:tw], reads=[b_h2s])
            P.barrier()
            P.emit(block)

        with ExitStack() as es:
          wup, b_wup = es.enter_context(nc.sbuf_tensor("wup", [128, 8, 2 * FFN], BF16)), Buf()
          wdn, b_wdn = es.enter_context(nc.sbuf_tensor("wdn", [128, FC, 1024], BF16)), Buf()
          with nc.Block() as block:
            for n0 in range(0, 2 * FFN, 1024):
                n1 = min(2 * FFN, n0 + 1024)
                P.dma("sp", wup[:, :, n0:n1], wup_b[:, :, n0:n1], writes=[b_wup])
            for k0 in range(0, FC, 8):
                k1 = min(FC, k0 + 8)
                P.dma("sp", wdn[:, k0:k1, :], wdn_b[:, k0:k1, :], writes=[b_wdn])
            FW = 256
            h2p = Pool(es, nc, "h2p", [128, KC, FW + 2], BF16, 2)
            x1p = Pool(es, nc, "x1f", [128, 2, D], F32, 1)
            gvp = Pool(es, nc, "gvp", [128, FW], F32, 8)
            ggp = Pool(es, nc, "ggp", [128, FC, FW], BF16, 1)
            outp = Pool(es, nc, "outp", [128, D], F32, 2)
            psU = Pool(es, nc, "psU", [128, 2, 512], F32, 3, psum=True)
            psD = Pool(es, nc, "psD", [128, 2, 512], F32, 1, psum=True)
            for tb in range(0, S2, FW):
                tw = min(FW, S2 - tb)
                nt3 = tw // 128
                h2, b_h2 = h2p.next()
                P.dma("sp", h2[:, :, 0:tw + 2], h2T_d[:, :, tb:tb + tw + 2], writes=[b_h2])
                x1, b_x1 = x1p.next()
                P.dma("sp", x1[:, 0:nt3, :], x1_d[:, tb // 128:tb // 128 + nt3, :], writes=[b_x1])
                gg, b_gg = ggp.next()
                prev_f = None

                def finish_f(f, res, gg, b_gg, tw):
                    (ga, b_ga), (va, b_va) = res
                    act(P, ga[:, 0:tw], ga[:, 0:tw], AF.Silu, [b_ga], [b_ga])
                    tt(P, "pool", gg[:, f, 0:tw], ga[:, 0:tw], va[:, 0:tw], ALU.mult, [b_ga, b_va], [b_gg])
                for f in range(FC):
                    pu, b_pu = psU.next()
                    for half, ch in ((0, f), (1, FC + f)):
                        mmg(P, pu[:, half, 0:tw + 2], [(wup[:, k, ch * 128:(ch + 1) * 128], h2[:, k, 0:tw + 2]) for k in range(KC)],
                            [b_wup, b_h2], [b_pu])
                    res = []
                    for half, ch in ((0, f), (1, FC + f)):
                        a, b_a_ = gvp.next()
                        act(P, a[:, 0:tw], pu[:, half, 0:tw], AF.Identity, [b_pu, b_fcw, b_fcb], [b_a_],
                            bias=fcb_t[:, ch:ch + 1], scale=fcw_t[:, ch, 0:1])
                        res.append((a, b_a_))
                    for tap in (1, 2):
                        for half, ch in ((0, f), (1, FC + f)):
                            a, b_a_ = res[half]
                            stt(P, "dve", a[:, 0:tw], pu[:, half, tap:tw + tap], fcw_t[:, ch, tap:tap + 1], a[:, 0:tw],
                                ALU.mult, ALU.add, [b_pu, b_fcw, b_a_], [b_a_])
                    if prev_f is not None:
                        finish_f(*prev_f)
                    prev_f = (f, res, gg, b_gg, tw)
                finish_f(*prev_f)
                prev_f = None
                for t3 in range(nt3):
                    pd, b_pd = psD.next()
                    for hf in range(2):
                        mmg(P, pd[:, hf, :], [(gg[:, f, t3 * 128:(t3 + 1) * 128], wdn[:, f, hf * 512:(hf + 1) * 512])
                                              for f in range(FC)], [b_gg, b_wdn], [b_pd])
                    ot, b_ot = outp.next()
                    tt(P, "dve", ot[:], pd[:].rearrange("p a b -> p (a b)"), x1[:, t3, :], ALU.add, [b_pd, b_x1], [b_ot])
                    r0 = tb + t3 * 128
                    P.dma("pool", out_d[r0:r0 + 128, :], ot[:], reads=[b_ot])
            if debug:
                P.barrier()
                for nm, src in (("dbg_hT", hT_d), ("dbg_qT", qT_d), ("dbg_kT", kT_d), ("dbg_v", v_d), ("dbg_dt", dt_d),
                                ("dbg_xs", xs_d), ("dbg_CT", CT_d), ("dbg_bt", bt_d), ("dbg_oT", oT_d),
                                ("dbg_ysT", ysT_d), ("dbg_yb", yb_d), ("dbg_x1", x1_d)):
                    P.dma("sp", dbg[nm], src)
            P.barrier()
            P.emit(block)
    return nc


def _t5_bucket_np(rel):
    half, max_exact = 16, 8
    bucket = np.where(rel > 0, half, 0)
    n = np.abs(rel)
    nf = np.maximum(n, 1).astype(np.float32)
    large = max_exact + (np.log(nf / np.float32(max_exact)) / np.float32(math.log(128 / max_exact))
                         * np.float32(half - max_exact)).astype(np.int32)
    large = np.minimum(large, half - 1)
    return bucket + np.where(n < max_exact, n, large)


def _core_inputs(inp, b, flip, S):
    f = np.float32
    c = lambda a: np.ascontiguousarray(a, dtype=f)
    x = inp["x"][b]
    if flip:
        x = x[::-1]
    w_in = inp["w_in"][0]
    o = 0
    cols = {}
    for nm, n in (("q", 1024), ("k", 1024), ("v", 1024), ("z", 2048), ("xbc", 4096), ("dt", 64), ("gate", 2048)):
        cols[nm] = w_in[:, o:o + n]
        o += n
    wdt = cols["dt"]
    dtb = np.concatenate([inp["dt_bias_f"][0], inp["dt_bias_b"][0]])
    alog = np.concatenate([inp["a_log_f"][0], inp["a_log_b"][0]])
    cw = inp["ssm_conv_w"][0]
    z1 = np.zeros((1, XBC), f)
    fw = inp["ffn_conv_w"][0]
    if flip:
        wdt = np.concatenate([wdt[:, 32:], wdt[:, :32]], axis=1)
        dtb = np.concatenate([dtb[32:], dtb[:32]])
        alog = np.concatenate([alog[32:], alog[:32]])
        cw5 = np.concatenate([z1, cw[::-1]], axis=0)
        fw3 = fw[::-1]
    else:
        cw5 = np.concatenate([cw, z1], axis=0)
        fw3 = fw
    sign = -1 if flip else 1
    kl = np.arange(128)[:, None]
    u = np.arange(1152)[None, :] - 512
    bidx = _t5_bucket_np(sign * (kl - u).astype(np.int32))
    rb = inp["rel_bias"]
    toep = np.transpose(rb[bidx], (0, 2, 1))
    lo = rb[_t5_bucket_np(np.array(sign * -1000, np.int32))]
    hi = rb[_t5_bucket_np(np.array(sign * 1000, np.int32))]
    farb = np.broadcast_to(np.stack([lo, hi], axis=1).reshape(1, 16), (128, 16))
    d = {
        "x": c(x),
        "w_q": c(cols["q"]), "w_k": c(cols["k"]), "w_v": c(cols["v"]), "w_z": c(cols["z"]),
        "w_xbc": c(cols["xbc"]), "w_dt": c(wdt), "w_gate": c(cols["gate"]),
        "w_ao": c(inp["w_attn_out"][0]), "w_so": c(inp["w_ssm_out"][0]), "w_o": c(inp["w_out"][0]),
        "w_up": c(inp["w_ffn_up"][0]), "w_dn": c(inp["w_ffn_down"][0]),
        "n1w": c(inp["norm1_w"]), "n2w": c(inp["norm2_w"]),
        "qkw": c(np.stack([np.tile(inp["q_norm_w"][0], 2), np.tile(inp["k_norm_w"][0], 2)], axis=1)),
        "sublnw": c(inp["subln_w"][0].reshape(128, 1)),
        "lamv": c(np.concatenate([inp["lambda_q1"][0], inp["lambda_k1"][0], inp["lambda_q2"][0],
                                  inp["lambda_k2"][0]]).reshape(1, 256)),
        "cw5": c(cw5.T.reshape(32, 128, 5).transpose(1, 0, 2)),
        "cb": c(inp["ssm_conv_b"][0].reshape(32, 128).T),
        "fcw": c(fw3.T.reshape(2 * FC, 128, 3).transpose(1, 0, 2)),
        "fcb": c(inp["ffn_conv_b"][0].reshape(2 * FC, 128).T),
        "dtb": c(dtb.reshape(1, 64)), "alog": c(alog.reshape(1, 64)),
        "dsk": c(inp["d_skip"][0].reshape(1, 32)), "snw": c(inp["ssm_norm_w"][0].reshape(1, SSMI)),
        "toep": c(toep), "farb": c(farb),
    }
    return d


_NC_CACHE = {}


def kernel(**inputs):
    inp = {k: np.asarray(v) for k, v in inputs.items()}
    Bn, S, _ = inp["x"].shape
    debug = bool(inp.pop("_debug", False)) if "_debug" in inp else False
    n = 2 * Bn
    key = (S, debug)
    if key not in _NC_CACHE:
        _NC_CACHE[key] = build_nc(S, lam_init=0.8 - 0.6 * math.exp(0.0), debug=debug)
    nc = _NC_CACHE[key]
    in_maps = [_core_inputs(inp, c // 2, c % 2, S) for c in range(n)]
    res = run_bass_kernel_spmd(nc, in_maps, core_ids=list(range(n)))
    S2 = S // 2
    out = np.empty((Bn, S, D), np.float32)
    for c in range(n):
        o = np.asarray(res.results[c]["out"])
        if c % 2 == 0:
            out[c // 2, :S2] = o
        else:
            out[c // 2, S2:] = o[::-1]
    if debug:
        kernel.last_results = res.results
    return out
```
